# Optimizing a Trainium2 kernel written in Bass

```python
import jax, jax.numpy as jnp
from jax import lax
import numpy as np

D_MODEL = 1024
BATCH = 2
SEQ = 8192
DEPTH = 2

CHUNK = 64
PLE_DIM = 256
EPS = 1e-6

D_SGU = 1024
SGU_BLOCK = 128
SGU_HEADS = 8
SGU_HEAD_DIM = D_SGU // SGU_HEADS

D_CONV = 1024
CONV_WIDTH = 31

D_POOL = 1024
POOL_WINDOWS = (2, 4, 8, 16)
POOL_GROUPS = len(POOL_WINDOWS)
POOL_GROUP_DIM = D_POOL // POOL_GROUPS

N_BRANCH = 3
D_IN = 2 * D_SGU + 2 * D_CONV + D_POOL + N_BRANCH * D_MODEL
SPLITS = (2 * D_SGU, 2 * D_SGU + 2 * D_CONV, 2 * D_SGU + 2 * D_CONV + D_POOL)

D_FF = -(-8 * D_MODEL // (3 * 256)) * 256

kernel_name = "hybrid_sgu_conformer_pool_block"


def rms_norm(x, g):
    xf = x.astype(jnp.float32)
    y = xf * lax.rsqrt(jnp.mean(xf * xf, axis=-1, keepdims=True) + EPS)
    return (y * g.astype(jnp.float32)).astype(x.dtype)


def layer_norm(x, g, b):
    xf = x.astype(jnp.float32)
    mu = jnp.mean(xf, axis=-1, keepdims=True)
    xc = xf - mu
    var = jnp.mean(xc * xc, axis=-1, keepdims=True)
    y = xc * lax.rsqrt(var + EPS) * g.astype(jnp.float32) + b.astype(jnp.float32)
    return y.astype(x.dtype)


def sgu_mask():
    c = jnp.arange(SGU_BLOCK) // CHUNK
    return c[None, :] <= c[:, None]


def spatial_gating(z, w_s, b_s, g_v, b_v):
    u, v = jnp.split(z, 2, axis=-1)
    v = layer_norm(v, g_v, b_v)
    bsz, s, _ = v.shape
    nb = s // SGU_BLOCK
    v = v.reshape(bsz, nb, SGU_BLOCK, SGU_HEADS, SGU_HEAD_DIM)
    w = jnp.where(sgu_mask()[None], w_s, jnp.zeros_like(w_s))
    mixed = jnp.einsum('hij,bnjhc->bnihc', w, v) + b_s.T[None, None, :, :, None]
    return u * mixed.reshape(bsz, s, D_SGU)


def conformer_conv(z, w_dw, b_dw, g_ln, b_ln):
    a, gate = jnp.split(z, 2, axis=-1)
    h = a * jax.nn.sigmoid(gate)
    h = lax.conv_general_dilated(
        h, w_dw, window_strides=(1,), padding=((CONV_WIDTH - 1, 0),),
        dimension_numbers=('NWC', 'WIO', 'NWC'),
        feature_group_count=D_CONV) + b_dw
    h = layer_norm(h, g_ln, b_ln)
    return jax.nn.silu(h)


def multiscale_pool(z, w_pool, s_pool):
    bsz, s, _ = z.shape
    zf = z.astype(jnp.float32)
    cs = jnp.cumsum(zf, axis=1)
    t = jnp.arange(1, s + 1, dtype=jnp.float32)
    outs = []
    for gi, w in enumerate(POOL_WINDOWS):
        sl = slice(gi * POOL_GROUP_DIM, (gi + 1) * POOL_GROUP_DIM)
        c = cs[..., sl]
        prev = jnp.pad(c[:, :s - w], ((0, 0), (w, 0), (0, 0)))
        cnt = jnp.minimum(t, float(w))[None, :, None]
        outs.append((c - prev) / cnt - zf[..., sl])
    pooled = jnp.stack(outs, axis=2).astype(z.dtype)
    mixed = jnp.einsum('bsgc,gcd->bsgd', pooled, w_pool)
    return mixed.reshape(bsz, s, D_POOL) * s_pool


def setup_inputs(seed: int = 0) -> dict:
    key = jax.random.key(seed)
    ks = jax.random.split(key, 32)
    f32 = jnp.float32

    def nrm(k, shape, scale):
        return jax.random.normal(k, shape, f32) * scale

    def gain(k, shape):
        return 1.0 + 0.05 * jax.random.normal(k, shape, f32)

    L = DEPTH
    return {
        "x": nrm(ks[0], (BATCH, SEQ, D_MODEL), 1.0),
        "p": nrm(ks[1], (DEPTH, BATCH, SEQ, PLE_DIM), 1.0),
        "g_mix_pre": gain(ks[2], (L, D_MODEL)),
        "w_in": nrm(ks[3], (L, D_MODEL, D_IN), D_MODEL ** -0.5),
        "w_sgu_s": nrm(ks[4], (L, SGU_HEADS, SGU_BLOCK, SGU_BLOCK), SGU_BLOCK ** -0.5),
        "b_sgu_s": 1.0 + 0.1 * jax.random.normal(ks[5], (L, SGU_HEADS, SGU_BLOCK), f32),
        "g_sgu_v": gain(ks[6], (L, D_SGU)),
        "b_sgu_v": nrm(ks[7], (L, D_SGU), 0.02),
        "w_sgu_out": nrm(ks[8], (L, D_SGU, D_MODEL), D_SGU ** -0.5),
        "w_dw": nrm(ks[9], (L, CONV_WIDTH, 1, D_CONV), CONV_WIDTH ** -0.5),
        "b_dw": nrm(ks[10], (L, D_CONV), 0.02),
        "g_conv_ln": gain(ks[11], (L, D_CONV)),
        "b_conv_ln": nrm(ks[12], (L, D_CONV), 0.02),
        "w_conv_out": nrm(ks[13], (L, D_CONV, D_MODEL), D_CONV ** -0.5),
        "w_pool": nrm(ks[14], (L, POOL_GROUPS, POOL_GROUP_DIM, POOL_GROUP_DIM), POOL_GROUP_DIM ** -0.5),
        "s_pool": 1.0 + 0.1 * jax.random.normal(ks[15], (L, D_POOL), f32),
        "w_pool_out": nrm(ks[16], (L, D_POOL, D_MODEL), D_POOL ** -0.5),
        "w_out": nrm(ks[17], (L, D_MODEL, D_MODEL), D_MODEL ** -0.5),
        "g_mix_post": gain(ks[18], (L, D_MODEL)),
        "g_ffn_pre": gain(ks[19], (L, D_MODEL)),
        "w_ffn_in": nrm(ks[20], (L, D_MODEL, 2 * D_FF), D_MODEL ** -0.5),
        "w_ffn_out": nrm(ks[21], (L, D_FF, D_MODEL), D_FF ** -0.5),
        "g_ffn_post": gain(ks[22], (L, D_MODEL)),
        "w_ple": nrm(ks[23], (L, PLE_DIM, D_MODEL), PLE_DIM ** -0.5),
        "w_ple_gate": nrm(ks[24], (L, D_MODEL, D_MODEL), D_MODEL ** -0.5),
    }


def reference(x, p, g_mix_pre, w_in, w_sgu_s, b_sgu_s, g_sgu_v, b_sgu_v, w_sgu_out,
              w_dw, b_dw, g_conv_ln, b_conv_ln, w_conv_out, w_pool, s_pool, w_pool_out,
              w_out, g_mix_post, g_ffn_pre, w_ffn_in, w_ffn_out, g_ffn_post,
              w_ple, w_ple_gate):
    h = x
    bsz, s, _ = x.shape
    for i in range(DEPTH):
        hn = rms_norm(h, g_mix_pre[i])
        proj = hn @ w_in[i]
        z_sgu, z_conv, z_pool, z_gate = jnp.split(proj, SPLITS, axis=-1)

        br_a = spatial_gating(jax.nn.gelu(z_sgu), w_sgu_s[i], b_sgu_s[i],
                              g_sgu_v[i], b_sgu_v[i]) @ w_sgu_out[i]
        br_b = conformer_conv(z_conv, w_dw[i], b_dw[i],
                              g_conv_ln[i], b_conv_ln[i]) @ w_conv_out[i]
        br_c = multiscale_pool(z_pool, w_pool[i], s_pool[i]) @ w_pool_out[i]

        gates = jax.nn.sigmoid(z_gate).reshape(bsz, s, N_BRANCH, D_MODEL)
        merged = gates[:, :, 0] * br_a + gates[:, :, 1] * br_b + gates[:, :, 2] * br_c
        h = h + rms_norm(merged @ w_out[i], g_mix_post[i])

        hn = rms_norm(h, g_ffn_pre[i])
        f_gate, f_up = jnp.split(hn @ w_ffn_in[i], 2, axis=-1)
        f = (jax.nn.silu(f_gate) * f_up) @ w_ffn_out[i]
        h = h + rms_norm(f, g_ffn_post[i])

        h = h + jax.nn.sigmoid(h @ w_ple_gate[i]) * (p[i] @ w_ple[i])
    return h
```

```python
import numpy as np
from contextlib import ExitStack
import concourse.bass as bass
import concourse.mybir as mybir
from concourse.bass_utils import run_bass_kernel_spmd

F32, BF16 = mybir.dt.float32, mybir.dt.bfloat16
AF = mybir.ActivationFunctionType
ALU = mybir.AluOpType

D = 1024
DFF = 2816
NOWN = 16
EPS = 1e-6
NSLOT = 3
RAW, WAW, WAR = 1, 2, 4
STRICT_SAME_ENGINE = True
MERGE_HALO = True
POOL_FROM_LAYER = 99
NPB = 80 + 8 * 31


ACT_TBL = {AF.Gelu_apprx_tanh: 1, AF.Sigmoid: 2, AF.Silu: 3, AF.Ln: 4, AF.Exp: 4}


class V:
    __slots__ = ("ap", "reg", "n")

    def __init__(self, ap, reg, n=512):
        self.ap = ap
        self.reg = reg
        self.n = n


class Buf:
    def __init__(self, name, t, es, cell=256):
        self.name, self.t, self.es, self.cell = name, t, es, cell

    def v(self, lo, hi, a=None):
        ap = self.t[:, lo:hi]
        if a:
            ap = ap.rearrange("p (a b) -> p a b", a=a)
        return V(ap, [(self.name, lo * self.es, hi * self.es, self.cell)], hi - lo)


class Op:
    __slots__ = ("eng", "fn", "reads", "writes", "dma", "chan", "deps", "sig", "need", "clock", "tag", "tbl")


TAG = [""]


class Prog:
    def __init__(self, nc):
        self.nc = nc
        self.ops = []

    def op(self, eng, fn, reads=(), writes=()):
        o = Op()
        o.eng, o.fn, o.reads, o.writes = eng, fn, list(reads), list(writes)
        o.dma, o.chan, o.need, o.sig, o.clock = False, None, False, None, None
        o.tag = TAG[0]
        o.tbl = None
        self.ops.append(o)
        return o

    def dma(self, eng, chan, fn, reads=(), writes=()):
        o = self.op(eng, fn, reads, writes)
        o.dma, o.chan = True, chan
        return o

    def schedule(self):
        import heapq
        ops = self.ops
        n = len(ops)
        cells = {}
        preds = [None] * n
        for i, X in enumerate(ops):
            deps = set()
            for v in X.reads:
                for (name, lo, hi, cell) in v.reg:
                    d = cells.setdefault(name, {})
                    for c in range(lo // cell, (hi - 1) // cell + 1):
                        st = d.get(c)
                        if st is not None and st[0] is not None:
                            deps.add(st[0])
            for v in X.writes:
                for (name, lo, hi, cell) in v.reg:
                    d = cells.setdefault(name, {})
                    for c in range(lo // cell, (hi - 1) // cell + 1):
                        st = d.get(c)
                        if st is not None:
                            if st[0] is not None:
                                deps.add(st[0])
                            deps.update(st[1])
            for v in X.reads:
                for (name, lo, hi, cell) in v.reg:
                    d = cells[name]
                    for c in range(lo // cell, (hi - 1) // cell + 1):
                        st = d.get(c)
                        if st is None:
                            d[c] = [None, {i}]
                        else:
                            st[1].add(i)
            for v in X.writes:
                for (name, lo, hi, cell) in v.reg:
                    d = cells[name]
                    for c in range(lo // cell, (hi - 1) // cell + 1):
                        d[c] = [i, set()]
            deps.discard(i)
            preds[i] = deps
        cost = [0.0] * n
        lat = [0.0] * n
        for i, X in enumerate(ops):
            nn = X.writes[0].n if X.writes else 512
            if X.dma:
                nb = 0
                for v in X.writes:
                    for (name, lo, hi, cell) in v.reg:
                        nb += (hi - lo) * (128 if cell != 1 else 8192 * 128)
                cost[i] = 0.08
                lat[i] = 2.0 + nb / 200e3
            elif X.eng == "PE":
                cost[i] = max(0.06, nn * 0.24 / 512)
            elif X.eng == "ACT":
                cost[i] = 0.19 + nn * 0.00083
            else:
                cost[i] = 0.16 + nn * 0.00104
        succs = [[] for _ in range(n)]
        indeg = [0] * n
        for i in range(n):
            indeg[i] = len(preds[i])
            for p in preds[i]:
                succs[p].append(i)
        bl = [0.0] * n
        for i in range(n - 1, -1, -1):
            m = 0.0
            for s_ in succs[i]:
                if bl[s_] > m:
                    m = bl[s_]
            bl[i] = m + cost[i] + lat[i]
        ready_t = [0.0] * n
        engs = ("PE", "ACT", "DVE", "POOL", "SP")
        rq = {e: [] for e in engs}
        free_t = {e: 0.0 for e in engs}
        for i in range(n):
            if indeg[i] == 0:
                heapq.heappush(rq[ops[i].eng], (-bl[i], i))
        start = [0.0] * n
        done = 0
        cur_tbl = [None]
        SEM = 0.3
        while done < n:
            best = None
            for e in engs:
                q = rq[e]
                if not q:
                    continue
                cand = None
                tfree = free_t[e]
                top = heapq.nsmallest(6, q)
                if e == "ACT":
                    for (nb_, i) in top:
                        if ready_t[i] <= tfree and (ops[i].tbl is None or ops[i].tbl == cur_tbl[0]):
                            cand = (tfree, nb_, i)
                            break
                if cand is None:
                    for (nb_, i) in top:
                        if ready_t[i] <= tfree:
                            cand = (tfree, nb_, i)
                            break
                if cand is None:
                    (nb_, i) = min(top, key=lambda x: (ready_t[x[1]], x[0]))
                    cand = (ready_t[i], nb_, i)
                if best is None or cand[0] < best[0][0]:
                    best = (cand, e)
            (t0, nb_, i), e = best
            rq[e].remove((nb_, i))
            heapq.heapify(rq[e])
            start[i] = t0
            fin = t0 + cost[i]
            if e == "ACT" and ops[i].tbl is not None:
                if ops[i].tbl != cur_tbl[0]:
                    fin += 1.3
                    cur_tbl[0] = ops[i].tbl
            free_t[e] = fin
            fin += lat[i]
            for s_ in succs[i]:
                if ready_t[s_] < fin + SEM:
                    ready_t[s_] = fin + SEM
                indeg[s_] -= 1
                if indeg[s_] == 0:
                    heapq.heappush(rq[ops[s_].eng], (-bl[s_], s_))
            done += 1
        order = sorted(range(n), key=lambda i: (start[i], i))
        self.ops = [ops[i] for i in order]
        return max(free_t.values())

    def finalize(self, stack):
        nc = self.nc
        ops = self.ops
        cells = {}
        for i, X in enumerate(ops):
            deps = {}
            for v in X.reads:
                for (name, lo, hi, cell) in v.reg:
                    d = cells.setdefault(name, {})
                    for c in range(lo // cell, (hi - 1) // cell + 1):
                        st = d.get(c)
                        if st is not None and st[0] is not None:
                            deps[st[0]] = deps.get(st[0], 0) | RAW
            for v in X.writes:
                for (name, lo, hi, cell) in v.reg:
                    d = cells.setdefault(name, {})
                    for c in range(lo // cell, (hi - 1) // cell + 1):
                        st = d.get(c)
                        if st is not None:
                            if st[0] is not None:
                                deps[st[0]] = deps.get(st[0], 0) | WAW
                            for r in st[1].values():
                                deps[r] = deps.get(r, 0) | WAR
            for v in X.reads:
                for (name, lo, hi, cell) in v.reg:
                    d = cells[name]
                    for c in range(lo // cell, (hi - 1) // cell + 1):
                        st = d.get(c)
                        if st is None:
                            d[c] = [None, {X.eng: i}]
                        else:
                            st[1][X.eng] = i
            for v in X.writes:
                for (name, lo, hi, cell) in v.reg:
                    d = cells[name]
                    for c in range(lo // cell, (hi - 1) // cell + 1):
                        d[c] = [i, {}]
            deps.pop(i, None)
            fd = {}
            for p, kind in deps.items():
                Pp = ops[p]
                if not Pp.dma and Pp.eng == X.eng and not X.dma:
                    if X.eng == "PE" or not (STRICT_SAME_ENGINE or (kind & RAW)):
                        continue
                if not Pp.dma and Pp.eng == X.eng and X.dma and not (kind & RAW):
                    continue
                fd[p] = kind
                Pp.need = True
            X.deps = fd
        sems = {}

        def sem(k):
            if k not in sems:
                sems[k] = stack.enter_context(nc.semaphore("s_" + k))
            return sems[k]

        engobj = {"PE": nc.tensor, "ACT": nc.scalar, "DVE": nc.vector, "POOL": nc.gpsimd, "SP": nc.sync}
        know = {k: {} for k in engobj}
        cnt = {}
        lastchan = {}
        nwait = 0
        for X in ops:
            F = X.eng
            K = know[F]
            waits = {}
            for p in X.deps:
                Pp = ops[p]
                k, v = Pp.sig
                if K.get(k, 0) < v:
                    waits[k] = max(waits.get(k, 0), v)
                for kk, vv in Pp.clock.items():
                    if K.get(kk, 0) < vv:
                        K[kk] = vv
            if X.dma:
                pv = lastchan.get(X.chan, 0)
                if K.get(X.chan, 0) < pv:
                    waits[X.chan] = max(waits.get(X.chan, 0), pv)
                    K[X.chan] = pv
                lastchan[X.chan] = pv + 16
                X.sig = (X.chan, pv + 16)
            elif X.need:
                cnt[F] = cnt.get(F, 0) + 1
                X.sig = (F, cnt[F])
            X.clock = dict(K)
            if X.sig:
                X.clock[X.sig[0]] = X.sig[1]
            e = engobj[F]
            wl = list(waits.items())
            nwait += len(wl)
            attach = None
            if wl and not X.dma and X.eng != "PE":
                attach = wl.pop()
            for (k, v) in wl:
                e.wait_ge(sem(k), v)
            ins = X.fn(e)
            if attach is not None:
                ins._wait_ge(sem(attach[0]), attach[1])
            if X.sig:
                ins.then_inc(sem(X.sig[0]), 16 if X.dma else 1)
        K = know["SP"]
        for ch, v in lastchan.items():
            nc.sync.wait_ge(sem(ch), v)
        return len(ops), nwait


def tile_specs():
    sp = {}
    order = []

    def add(name, kc, ncc, segs):
        sp[name] = (kc, ncc, segs)
        order.append(name)

    for i in range(4):
        add(f"B1_{i}", 8, 512, [("w_in", 0, 2048 + 256 * i, 256, 0), ("w_in", 0, 3072 + 256 * i, 256, 256)])
    for i in range(2):
        add(f"ZP_{i}", 8, 512, [("w_in", 0, 4096 + 512 * i, 512, 0)])
    for i in range(2):
        add(f"U_{i}", 8, 512, [("w_in", 0, 512 * i, 512, 0)])
    for i in range(2):
        add(f"V_{i}", 8, 512, [("w_in", 0, 1024 + 512 * i, 512, 0)])
    for i in range(2):
        add(f"G0_{i}", 8, 512, [("w_in", 0, 5120 + 512 * i, 512, 0)])
    for i in range(2):
        add(f"SGUOUT_{i}", 8, 512, [("w_sgu_out", 0, 512 * i, 512, 0)])
    add("WPOOL", 2, 1024, [("w_pool", g, 0, 256, 256 * g) for g in range(4)])
    for i in range(2):
        add(f"G2_{i}", 8, 512, [("w_in", 0, 7168 + 512 * i, 512, 0)])
    for i in range(2):
        add(f"POOLOUT_{i}", 8, 512, [("w_pool_out", 0, 512 * i, 512, 0)])
    for i in range(2):
        add(f"G1_{i}", 8, 512, [("w_in", 0, 6144 + 512 * i, 512, 0)])
    for i in range(2):
        add(f"CONVOUT_{i}", 8, 512, [("w_conv_out", 0, 512 * i, 512, 0)])
    for i in range(2):
        add(f"WOUT_{i}", 8, 512, [("w_out", 0, 512 * i, 512, 0)])
    for i in range(11):
        add(f"FFNIN_{i}", 8, 512, [("w_ffn_in", 0, 256 * i, 256, 0), ("w_ffn_in", 0, DFF + 256 * i, 256, 256)])
    for cg in range(2):
        for kg, (k0, kc) in enumerate(((0, 8), (8, 8), (16, 6))):
            add(f"FFNOUT_{cg}_{kg}", kc, 512, [("w_ffn_out", k0 * 128, 512 * cg, 512, 0)])
    for i in range(2):
        add(f"PLEG_{i}", 8, 512, [("w_ple_gate", 0, 512 * i, 512, 0)])
    add("PLEW", 2, 1024, [("w_ple", 0, 0, 1024, 0)])
    return sp, order


WSHAPES = {
    "w_in": [D, 8192], "w_sgu_out": [D, D], "w_conv_out": [D, D], "w_pool": [4, 256, 256],
    "w_pool_out": [D, D], "w_out": [D, D], "w_ffn_in": [D, 2 * DFF], "w_ffn_out": [DFF, D],
    "w_ple": [256, D], "w_ple_gate": [D, D],
}


def build(nl, last, dbg=False):
    HB = nl
    NBT = HB + NOWN
    NTOK = NBT * 128
    nc = bass.Bass("TRN2", target_bir_lowering=False)
    P = Prog(nc)
    specs, order = tile_specs()
    NTL = len(order)
    tidx = {(l, n): l * NTL + i for l in range(nl) for i, n in enumerate(order)}

    din = {}

    def dram_in(name, shape):
        din[name] = nc.dram_tensor(name, shape, F32, kind="ExternalInput").ap()
        return din[name]

    xT = dram_in("xT", [8, 128, NTOK])
    pT = dram_in("pT", [nl, 2, 128, NTOK])
    for wn, shp in WSHAPES.items():
        dram_in(wn, [nl] + shp)
    pblob_d = dram_in("pblob", [nl, 128, NPB])
    wsT_d = dram_in("wsT", [nl, 128, 1024])
    bsb_d = dram_in("bsb", [nl, 128, 1024])
    maskT_d = dram_in("maskT", [128, 128])
    pmats_d = dram_in("pmats", [128, 16 * 128])
    cmask_d = dram_in("cmask", [128, 1])
    outT = nc.dram_tensor("outT", [8, 128, NOWN * 128], F32, kind="ExternalOutput").ap()
    wscr = nc.dram_tensor("wscr", [nl * NTL, 128, 4096], BF16).ap()

    with ExitStack() as st:
        def sb(name, n, dt, cell=256):
            t = st.enter_context(nc.sbuf_tensor(name, [128, n], dt))
            return Buf(name, t, 4 if dt == F32 else 2, cell)

        H = sb("H", 8 * NTOK, F32)
        WS = sb("WS", NSLOT * 4096, BF16)
        HN = sb("HN", 8 * 512, BF16)
        AB = sb("AB", 12 * 1024, BF16)
        ZPV = sb("ZPV", 1024, BF16)
        G = sb("G", 8 * 512, BF16)
        M = sb("M", 8 * 512, F32)
        GLU = sb("GLU", 8 * 542, F32, 64)
        NH = sb("NH", 8 * 30, F32, 64)
        TMP = sb("TMP", 3 * 512, F32)
        SQB = sb("SQB", 2 * 512, BF16)
        CACC = sb("CACC", 2 * 512, F32)
        WST = sb("WST", 1024, BF16)
        CC = sb("CC", 1024, F32)
        PB = sb("PB", nl * NPB, F32, 64)
        PM = sb("PM", 16 * 128, BF16)
        PMF = None
        MASK = sb("MASK", 128, F32)
        ONES = sb("ONES", 128, BF16)
        SM = sb("SM", 128, F32, 4)
        banks = []
        for i in range(8):
            t = st.enter_context(nc.psum_tensor(f"ps{i}", [128, 512], F32))
            banks.append(Buf(f"ps{i}", t, 4, 2048))
        bank_i = [0]

        def bank():
            b = banks[bank_i[0] % 6]
            bank_i[0] += 1
            return b

        tmp_i = [0]

        def tmp(T):
            i = tmp_i[0] % 3
            tmp_i[0] += 1
            return TMP.v(i * 512, i * 512 + T)

        sq_i = [0]

        def sqb(T):
            i = sq_i[0] % 2
            sq_i[0] += 1
            return SQB.v(i * 512, i * 512 + T)

        ABF = Buf("AB", AB.t, 2, 256)

        def ab_bf(lo_b, n):
            return AB.v(lo_b // 2, lo_b // 2 + n)

        ABf32_t = AB.t

        VG = sb("VG", 2 * 1024, F32)
        PF = VG

        class Sub:
            def __init__(self, buf, off):
                self.buf, self.off, self.t = buf, off, buf.t

            def v(self, lo, hi):
                return self.buf.v(self.off + lo, self.off + hi)
        MEAN = Sub(VG, 0)
        RSTD = Sub(VG, 512)
        PUMPN = [1]
        bgq = []
        pump_ctr = [0]

        def pump(n):
            for _ in range(n * PUMPN[0]):
                for ent in list(bgq):
                    g, rate = ent
                    for _r in range(rate):
                        try:
                            next(g)
                        except StopIteration:
                            bgq.remove(ent)
                            break

        def flush():
            while bgq:
                pump(1)

        def rd(x):
            return [x] if isinstance(x, V) else []

        def apf(x):
            return x.ap if isinstance(x, V) else x

        def act(out, in_, func, bias=None, scale=None):
            kw = {}
            if bias is not None:
                kw["bias"] = apf(bias)
            if scale is not None:
                kw["scale"] = apf(scale)
            o = P.op("ACT", lambda e: e.activation(out=out.ap, in_=in_.ap, func=func, **kw),
                     reads=[in_] + rd(bias) + rd(scale), writes=[out])
            o.tbl = ACT_TBL.get(func)

        def tt(out, a, b, op, eng="DVE"):
            P.op(eng, lambda e: e.tensor_tensor(out=out.ap, in0=a.ap, in1=b.ap, op=op), reads=[a, b], writes=[out])

        def stt(out, in0, scalar, in1, op0, op1, eng="DVE"):
            P.op(eng, lambda e: e.scalar_tensor_tensor(out=out.ap, in0=in0.ap, scalar=apf(scalar), in1=in1.ap,
                                                       op0=op0, op1=op1),
                 reads=[in0, in1] + rd(scalar), writes=[out])

        def ts(out, in0, s1, s2, op0, op1=None, eng="DVE"):
            if op1 is None:
                P.op(eng, lambda e: e.tensor_scalar(out=out.ap, in0=in0.ap, scalar1=apf(s1), scalar2=None, op0=op0),
                     reads=[in0] + rd(s1), writes=[out])
            else:
                P.op(eng, lambda e: e.tensor_scalar(out=out.ap, in0=in0.ap, scalar1=apf(s1), scalar2=apf(s2),
                                                    op0=op0, op1=op1),
                     reads=[in0] + rd(s1) + rd(s2), writes=[out])

        def cp(out, in_, eng="DVE"):
            P.op(eng, lambda e: e.tensor_copy(out=out.ap, in_=in_.ap), reads=[in_], writes=[out])

        def recip(out, in_):
            P.op("DVE", lambda e: e.reciprocal(out=out.ap, in_=in_.ap), reads=[in_], writes=[out])

        def mm(out, lhsT, rhs, start, stop):
            P.op("PE", lambda e: e.matmul(out.ap, lhsT=lhsT.ap, rhs=rhs.ap, start=start, stop=stop),
                 reads=[lhsT, rhs], writes=[out])

        def load(chan, out, src_ap, src_reg=None):
            P.dma("SP", chan, lambda e: e.dma_start(out=out.ap, in_=src_ap),
                  reads=[V(None, src_reg)] if src_reg else [], writes=[out])

        def pcol(l, j, n=1):
            return PB.v(l * NPB + j, l * NPB + j + n)

        PG_MIXPRE, PG_SGUV, PB_SGUV, PB_DW, PG_CLN, PB_CLN, PS_POOL, PG_MIXPOST, PG_FFNPRE, PG_FFNPOST, PW_DW = (
            0, 8, 16, 24, 32, 40, 48, 56, 64, 72, 80)
        EPSV = SM.v(0, 1)
        CMASK = SM.v(1, 2)

        P.op("DVE", lambda e: e.memset(ONES.t[:, :], 1.0), writes=[ONES.v(0, 128)])
        P.op("DVE", lambda e: e.memset(SM.t[:, 0:1], EPS), writes=[EPSV])
        load("misc", CMASK, cmask_d[:, :])
        for l in range(nl):
            load("misc", PB.v(l * NPB, (l + 1) * NPB), pblob_d[l])
        load("misc", MASK.v(0, 128), maskT_d[:, :])
        for hh in range(2):
            load("misc", VG.v(1024, 2048), pmats_d[:, hh * 1024:(hh + 1) * 1024])
            cp(PM.v(hh * 1024, (hh + 1) * 1024), VG.v(1024, 2048))
        xsplit = (HB + 3) * 128
        for c in range(8):
            load(f"x{c % 4}", H.v(c * NTOK, c * NTOK + xsplit), xT[c][:, 0:xsplit])
        for c in range(8):
            load(f"x{c % 4}", H.v(c * NTOK + xsplit, (c + 1) * NTOK), xT[c][:, xsplit:NTOK])
        ci = 0
        for l in range(nl):
            for n in order:
                kc, ncc, segs = specs[n]
                ti = tidx[(l, n)]
                dst3 = wscr[ti][:, 0:kc * ncc].rearrange("p (k c) -> p k c", k=kc)
                for si, (wn, r0, c0, ncols, doff) in enumerate(segs):
                    if wn == "w_pool":
                        src = din[wn][l, r0, :, c0:c0 + ncols]
                    else:
                        src = din[wn][l, r0:r0 + kc * 128, c0:c0 + ncols]
                    src3 = src.rearrange("(k p) c -> p k c", p=128)
                    d3 = dst3[:, :, doff:doff + ncols]
                    P.dma("POOL", f"cast{ci % 6}", (lambda e, d3=d3, src3=src3: e.dma_start(out=d3, in_=src3)),
                          writes=[V(None, [("wscr", ti * 4 + si, ti * 4 + si + 1, 1)])])
                    ci += 1

        def dump(name, buf, lo, hi, on):
            if not (dbg and on):
                return
            dt = BF16 if buf.es == 2 else F32
            d = nc.dram_tensor("dbg_" + name, [128, hi - lo], dt, kind="ExternalOutput").ap()
            v = buf.v(lo, hi)
            P.dma("SP", "dbg", (lambda e, d=d, v=v: e.dma_start(out=d[:, :], in_=v.ap)), reads=[v])

        slot_i = [0]

        def wtile(l, name):
            kc, ncc, segs = specs[name]
            TAG[0] = name.split('_')[0]
            ti = tidx[(l, name)]
            s = slot_i[0] % NSLOT
            slot_i[0] += 1
            dst = WS.v(s * 4096, s * 4096 + kc * ncc)
            load(f"w{s}", dst, wscr[ti][:, 0:kc * ncc], [("wscr", ti * 4, ti * 4 + 4, 1)])

            def w(k, c0, n=128):
                o = s * 4096 + k * ncc + c0
                return WS.v(o, o + n)
            return w

        def hv(c, b0, nb):
            o = c * NTOK + (b0 + HB) * 128
            return H.v(o, o + nb * 128)

        def hn(c, T, t0=0):
            return HN.v(c * 512 + t0, c * 512 + T)

        def rstd_of(out, in_, scale):
            act(out, in_, AF.Ln, bias=EPSV, scale=scale)
            act(out, out, AF.Exp, scale=-0.5)

        def rstd_from_bank(S, T):
            rstd_of(RSTD.v(0, T), S.v(0, T), 1.0 / D)

        def rms_to_hn(l, b0, nb, gcol, dst=None):
            dst = dst or HN
            T = nb * 128
            TAG[0] = 'rms'
            S = banks[7]
            for c in range(8):
                q = sqb(T)
                act(q, hv(c, b0, nb), AF.Square)
                mm(S.v(0, T), ONES.v(0, 128), q, c == 0, c == 7)
            rstd_from_bank(S, T)
            for c in range(8):
                stt(dst.v(c * 512, c * 512 + T), hv(c, b0, nb), pcol(l, gcol + c), RSTD.v(0, T), ALU.mult, ALU.mult)

        def fm_group(w, kc, oc, rhs_fn, T):
            b = bank()
            for k in range(kc):
                mm(b.v(0, T), w(k, oc * 128), rhs_fn(k), k == 0, k == kc - 1)
            pump(1)
            return b

        def b1_tiles(l, T, dst_fn):
            for i in range(4):
                w = wtile(l, f"B1_{i}")
                for j in range(2):
                    c = 2 * i + j
                    gb = fm_group(w, 8, 2 + j, lambda k: hn(k, T), T)
                    sg = tmp(T)
                    act(sg, gb.v(0, T), AF.Sigmoid)
                    ab_ = fm_group(w, 8, j, lambda k: hn(k, T), T)
                    dst_fn(c, ab_, sg)

        def zp_tiles(l, nb, slot0):
            T = nb * 128
            for i in range(2):
                w = wtile(l, f"ZP_{i}")
                for b in range(nb):
                    pb_ = bank()
                    for k in range(8):
                        mm(pb_.v(0, 512), hn(k, b * 128 + 128, b * 128), w(k, 0, 512), k == 0, k == 7)
                    o = zp_off(slot0 + b) + i * 512
                    act(AB.v(o, o + 512), pb_.v(0, 512), AF.Identity)
                    pump(1)

        def zp_off(slot):
            return 4096 + (slot - 1) * 1024

        def zp_view(slot, c):
            if slot == 0:
                return ZPV.v(c * 128, c * 128 + 128)
            o = zp_off(slot) + c * 128
            return AB.v(o, o + 128)

        def sgu_setup(l):
            load("misc", VG.v(1024, 2048), wsT_d[l])
            mask_bc = V(bass.AP(MASK.t, 0, [[128, 128], [0, 8], [1, 128]]), MASK.v(0, 128).reg)
            P.op("DVE", lambda e: e.tensor_tensor(out=WST.t[:, :].rearrange("p (h i) -> p h i", h=8),
                                                  in0=VG.t[:, 1024:2048].rearrange("p (h i) -> p h i", h=8),
                                                  in1=mask_bc.ap, op=ALU.mult),
                 reads=[VG.v(1024, 2048), mask_bc], writes=[WST.v(0, 1024)])
            rb = [bank(), bank()]
            for i in range(2):
                mm(rb[i].v(0, 512), ONES.v(0, 128), WST.v(i * 512, i * 512 + 512), True, True)
            load("misc", VG.v(1024, 2048), bsb_d[l])
            for h in range(8):
                stt(CC.v(h * 128, h * 128 + 128), rb[h // 4].v((h % 4) * 128, (h % 4) * 128 + 128),
                    pcol(l, PB_SGUV + h), VG.v(1024 + h * 128, 1024 + h * 128 + 128), ALU.mult, ALU.add)

        state = {"zlast": 4}

        def partial_block(l, pb):
            rms_to_hn(l, pb, 1, PG_MIXPRE)

            def dst(c, a_bank, sg):
                tt(NH.v(c * 30, c * 30 + 30), a_bank.v(98, 128), V(sg.ap[:, 98:128], sg.reg), ALU.mult)
            b1_tiles(l, 128, dst)
            zp_tiles(l, 1, 4)
            state["zlast"] = 4

        def post_norm_residual(l, b0, nb, gcol, producer):
            T = nb * 128
            S = banks[6]
            pend = []
            for c in range(8):
                yb_ = producer(c)
                q = sqb(T)
                act(q, yb_.v(0, T), AF.Square)
                act(M.v(c * 512, c * 512 + T), yb_.v(0, T), AF.Identity)
                pend.append((c, q))
                if len(pend) > 1:
                    cc_, qq = pend.pop(0)
                    mm(S.v(0, T), ONES.v(0, 128), qq, cc_ == 0, False)
            cc_, qq = pend.pop(0)
            mm(S.v(0, T), ONES.v(0, 128), qq, False, True)
            rstd_from_bank(S, T)
            for c in range(8):
                y = M.v(c * 512, c * 512 + T)
                tt(y, y, RSTD.v(0, T), ALU.mult)
                stt(hv(c, b0, nb), y, pcol(l, gcol + c), hv(c, b0, nb), ALU.mult, ALU.add)

        def pre_rms(l, b0, nb):
            rms_to_hn(l, b0, nb, PG_MIXPRE)

        def pre_b1(l, b0, nb):
            T = nb * 128
            nh3 = V(NH.t[:, :].rearrange("p (c n) -> p c n", c=8), NH.v(0, 240).reg)
            gl3 = V(GLU.t[:, :].rearrange("p (c n) -> p c n", c=8)[:, :, 0:30], GLU.v(0, 8 * 542).reg)
            if b0 == 0:
                ts(gl3, nh3, CMASK, None, ALU.mult)
            else:
                cp(gl3, nh3)
            if state["zlast"] != 0:
                zl = state["zlast"]
                cp(ZPV.v(0, 1024), AB.v(zp_off(zl), zp_off(zl) + 1024))
            def glu_dst(c, a_bank, sg):
                tt(GLU.v(c * 542 + 30, c * 542 + 30 + T), a_bank.v(0, T), sg, ALU.mult)
            dz = (l == 0 and b0 == 0)
            dump('hn', HN, 0, 4096, dz)
            b1_tiles(l, T, glu_dst)
            dump('glu', GLU, 0, 8 * 542, dz)

        def conv_gen(l, b0, nb, chunks, eng, slot):
            T = nb * 128
            a = CACC.v(slot * 512, slot * 512 + T)
            pr = CACC.v((slot + 1) * 512, (slot + 1) * 512 + T) if eng != "DVE" else None
            for c in chunks:
                g0 = c * 542
                for k in range(31):
                    wk = pcol(l, PW_DW + c * 31 + k)
                    if k == 0:
                        ts(a, GLU.v(g0, g0 + T), wk, pcol(l, PB_DW + c), ALU.mult, ALU.add, eng=eng)
                    elif k < 30:
                        if eng == "DVE":
                            stt(a, GLU.v(g0 + k, g0 + k + T), wk, a, ALU.mult, ALU.add)
                        else:
                            ts(pr, GLU.v(g0 + k, g0 + k + T), wk, None, ALU.mult, eng=eng)
                            yield
                            tt(a, a, pr, ALU.add, eng=eng)
                    else:
                        cp(NH.v(c * 30, c * 30 + 30), GLU.v(g0 + T, g0 + T + 30), eng=eng)
                        co_ = GLU.v(g0 + 30, g0 + 30 + T)
                        if eng == "DVE":
                            stt(co_, co_, wk, a, ALU.mult, ALU.add)
                        else:
                            ts(pr, co_, wk, None, ALU.mult, eng=eng)
                            yield
                            tt(co_, a, pr, ALU.add, eng=eng)
                    yield

        def start_conv(l, b0, nb):
            assert not bgq
            if l >= POOL_FROM_LAYER:
                bgq.append([conv_gen(l, b0, nb, (0, 1, 2, 3, 4, 5), "DVE", 0), 2])
                bgq.append([conv_gen(l, b0, nb, (6, 7), "POOL", 1), 1])
            else:
                bgq.append([conv_gen(l, b0, nb, (0, 1, 2, 3), "DVE", 0), 1])
                bgq.append([conv_gen(l, b0, nb, (4, 5, 6, 7), "DVE", 1), 1])

        def mixer_uv(l, b0, nb):
            T = nb * 128
            for i in range(2):
                w = wtile(l, f"U_{i}")
                for j in range(4):
                    c = 4 * i + j
                    b = fm_group(w, 8, j, lambda k: hn(k, T), T)
                    act(AB.v(c * 512, c * 512 + T), b.v(0, T), AF.Gelu_apprx_tanh)
            wv = [None, None]
            MV = SM.v(8, 8 + 2 * nb)
            for b in range(nb):
                vgo = (b % 2) * 1024
                for i in range(2):
                    if b == 0:
                        wv[i] = wtile(l, f"V_{i}")
                for i in range(2):
                    pb_ = bank()
                    for k in range(8):
                        mm(pb_.v(0, 512), hn(k, b * 128 + 128, b * 128), wv[i](k, 0, 512), k == 0, k == 7)
                    act(VG.v(vgo + i * 512, vgo + i * 512 + 512), pb_.v(0, 512), AF.Gelu_apprx_tanh)
                    pump(1)
                    bs = SM.v(32 + 6 * i, 38 + 6 * i)
                    P.op("DVE", lambda e, o=bs, s=VG.v(vgo + i * 512, vgo + i * 512 + 512): e.bn_stats(out=o.ap, in_=s.ap),
                         reads=[VG.v(vgo + i * 512, vgo + i * 512 + 512)], writes=[bs])
                mvb = SM.v(8 + 2 * b, 10 + 2 * b)
                P.op("DVE", lambda e, o=mvb, s=SM.v(32, 44): e.bn_aggr(out=o.ap, in_=s.ap), reads=[SM.v(32, 44)], writes=[mvb])
                rs = SM.v(48 + b, 49 + b)
                nm = SM.v(56 + b, 57 + b)
                rstd_of(rs, SM.v(9 + 2 * b, 10 + 2 * b), 1.0)
                stt(nm, SM.v(8 + 2 * b, 9 + 2 * b), -1.0, rs, ALU.mult, ALU.mult)
                vh = AB.v(4096 + b * 1024, 4096 + (b + 1) * 1024)
                act(vh, VG.v(vgo, vgo + 1024), AF.Identity, bias=nm, scale=rs)

        def mixer_rest(l, b0, nb, hook):
            T = nb * 128
            PUMPN[0] = 1
            meng = "POOL" if l >= POOL_FROM_LAYER else "DVE"
            dz = (l == 0 and b0 == 0)
            for i in range(2):
                w = wtile(l, f"G0_{i}")
                for j in range(4):
                    c = 4 * i + j
                    b = fm_group(w, 8, j, lambda k: hn(k, T), T)
                    act(G.v(c * 512, c * 512 + T), b.v(0, T), AF.Sigmoid)
            dump('u', AB, 0, 4096, dz)
            dump('vhat', AB, 4096, 8192, dz)
            dump('g0', G, 0, 4096, dz)
            cc_bc = lambda h: V(bass.AP(CC.t, h * 128, [[1024, 128], [0, nb], [1, 128]]), CC.v(h * 128, h * 128 + 128).reg)
            TAG[0] = 'sgumm'
            for h in range(8):
                mb_ = bank()
                for b in range(nb):
                    vh = AB.v(4096 + b * 1024 + h * 128, 4096 + b * 1024 + h * 128 + 128)
                    mm(mb_.v(b * 128, b * 128 + 128), vh, WST.v(h * 128, h * 128 + 128), True, True)
                t_ = tmp(T)
                cb = cc_bc(h)
                t3 = V(t_.ap.rearrange("p (b i) -> p b i", b=nb), t_.reg)
                m3 = V(mb_.v(0, T).ap.rearrange("p (b i) -> p b i", b=nb), mb_.v(0, T).reg)
                stt(t3, m3, pcol(l, PG_SGUV + h), cb, ALU.mult, ALU.add)
                u = AB.v(h * 512, h * 512 + T)
                tt(u, t_, u, ALU.mult)
            for i in range(2):
                w = wtile(l, f"SGUOUT_{i}")
                for j in range(4):
                    c = 4 * i + j
                    b = fm_group(w, 8, j, lambda k: AB.v(k * 512, k * 512 + T), T)
                    tt(M.v(c * 512, c * 512 + T), b.v(0, T), G.v(c * 512, c * 512 + T), ALU.mult)
            dump('sgo', AB, 0, 4096, dz)
            dump('m_a', M, 0, 4096, dz)
            zp_tiles(l, nb, 1)
            pooled = lambda c: AB.v(c * 512, c * 512 + T)
            TAG[0] = 'poolmm'
            for c in range(8):
                g = c // 2
                pb_ = bank()
                for b in range(nb):
                    first = (b0 + b == 0)
                    kc_, kp_ = (2, 3) if first else (0, 1)
                    pc = PM.v((kc_ * 4 + g) * 128, (kc_ * 4 + g) * 128 + 128)
                    pp = PM.v((kp_ * 4 + g) * 128, (kp_ * 4 + g) * 128 + 128)
                    mm(pb_.v(b * 128, b * 128 + 128), zp_view(b + 1, c), pc, True, False)
                    mm(pb_.v(b * 128, b * 128 + 128), zp_view(b, c), pp, False, True)
                act(pooled(c), pb_.v(0, T), AF.Identity)
                pump(1)
            dump('zp', AB, 4096, 8192, dz)
            dump('pooled', AB, 0, 4096, dz)
            state["zlast"] = nb
            w = wtile(l, "WPOOL")
            plo = lambda c: AB.v(8192 + c * 512, 8192 + c * 512 + T)
            for g in range(4):
                for hh in range(2):
                    pb_ = bank()
                    for k in range(2):
                        mm(pb_.v(0, T), w(k, g * 256 + hh * 128), pooled(2 * g + k), k == 0, k == 1)
                    c = 2 * g + hh
                    act(plo(c), pb_.v(0, T), AF.Identity, scale=pcol(l, PS_POOL + c))
            for i in range(2):
                w = wtile(l, f"G2_{i}")
                for j in range(4):
                    c = 4 * i + j
                    b = fm_group(w, 8, j, lambda k: hn(k, T), T)
                    act(G.v(c * 512, c * 512 + T), b.v(0, T), AF.Sigmoid)
            cp(ZPV.v(0, 1024), AB.v(zp_off(nb), zp_off(nb) + 1024))
            state["zlast"] = 0
            flush()
            dump('co', GLU, 0, 8 * 542, dz)
            S1, S2 = banks[6], banks[7]
            co = lambda c: GLU.v(c * 542 + 30, c * 542 + 30 + T)
            cvo = lambda c: AB.v(c * 512, c * 512 + T)
            mean = MEAN.v(0, T)
            for i in range(2):
                w = wtile(l, f"POOLOUT_{i}")
                for j in range(4):
                    c = 4 * i + j
                    q1 = sqb(T)
                    act(q1, co(c), AF.Identity)
                    q2 = sqb(T)
                    act(q2, co(c), AF.Square)
                    TAG[0] = 'POOLOUT'
                    b = fm_group(w, 8, j, lambda k: plo(k), T)
                    TAG[0] = 'lnstat'
                    mm(S1.v(0, T), ONES.v(0, 128), q1, c == 0, c == 7)
                    mm(S2.v(0, T), ONES.v(0, 128), q2, c == 0, c == 7)
                    t_ = tmp(T)
                    tt(t_, b.v(0, T), G.v(c * 512, c * 512 + T), ALU.mult)
                    tt(M.v(c * 512, c * 512 + T), M.v(c * 512, c * 512 + T), t_, ALU.add, eng=meng)
            dump('plo', AB, 8192, 12288, dz)
            dump('m_c', M, 0, 4096, dz)
            ts(mean, S1.v(0, T), 1.0 / D, None, ALU.mult)
            msq = tmp(T)
            tt(msq, mean, mean, ALU.mult)
            var = tmp(T)
            stt(var, S2.v(0, T), 1.0 / D, msq, ALU.mult, ALU.subtract)
            rstd_of(RSTD.v(0, T), var, 1.0)
            for i in range(2):
                w = wtile(l, f"G1_{i}")
                for j in range(4):
                    c = 4 * i + j
                    b = fm_group(w, 8, j, lambda k: hn(k, T), T)
                    act(G.v(c * 512, c * 512 + T), b.v(0, T), AF.Sigmoid)
                    tt(co(c), co(c), mean, ALU.subtract)
                    tt(co(c), co(c), RSTD.v(0, T), ALU.mult)
                    act(cvo(c), co(c), AF.Silu, bias=pcol(l, PB_CLN + c), scale=pcol(l, PG_CLN + c))
            if hook is not None:
                hook()
            mbv = lambda c: AB.v(4096 + c * 512, 4096 + c * 512 + T)
            for i in range(2):
                w = wtile(l, f"CONVOUT_{i}")
                for j in range(4):
                    c = 4 * i + j
                    b = fm_group(w, 8, j, lambda k: cvo(k), T)
                    t_ = tmp(T)
                    tt(t_, b.v(0, T), G.v(c * 512, c * 512 + T), ALU.mult)
                    tt(t_, M.v(c * 512, c * 512 + T), t_, ALU.add, eng=meng)
                    act(mbv(c), t_, AF.Identity)
            dump('cvo', AB, 0, 4096, dz)
            dump('mb', AB, 4096, 8192, dz)
            wo = [None, None]

            def prod_wout(c):
                i, j = c // 4, c % 4
                if j == 0:
                    wo[i] = wtile(l, f"WOUT_{i}")
                return fm_group(wo[i], 8, j, lambda k: mbv(k), T)
            post_norm_residual(l, b0, nb, PG_MIXPOST, prod_wout)
            for c in range(8):
                dump(f'h1_{c}', H, c * NTOK + HB * 128, c * NTOK + HB * 128 + 512, dz)

        def ffn(l, b0, nb):
            T = nb * 128
            dz = (l == 0 and b0 == 0)
            meng = "POOL" if l >= POOL_FROM_LAYER else "DVE"
            PUMPN[0] = 1
            for k in range(2):
                load("p", VG.v(1024 + k * 512, 1024 + k * 512 + T), pT[l, k][:, (b0 + HB) * 128:(b0 + HB) * 128 + T])
                cp(AB.v(11264 + k * 512, 11264 + k * 512 + T), VG.v(1024 + k * 512, 1024 + k * 512 + T))
            rms_to_hn(l, b0, nb, PG_FFNPRE, dst=G)
            hn2 = lambda k: G.v(k * 512, k * 512 + T)
            actv = lambda j: AB.v(j * 512, j * 512 + T)
            for i in range(11):
                w = wtile(l, f"FFNIN_{i}")
                for j in range(2):
                    gb = fm_group(w, 8, j, hn2, T)
                    sg = tmp(T)
                    act(sg, gb.v(0, T), AF.Silu)
                    ub = fm_group(w, 8, 2 + j, hn2, T)
                    tt(actv(2 * i + j), ub.v(0, T), sg, ALU.mult)
            fb = {}

            def prod_ffn(c):
                cg, j = c // 4, c % 4
                if j == 0:
                    bk = [bank() for _ in range(4)]
                    for kg, (k0, kc) in enumerate(((0, 8), (8, 8), (16, 6))):
                        w = wtile(l, f"FFNOUT_{cg}_{kg}")
                        for jj in range(4):
                            for k in range(kc):
                                mm(bk[jj].v(0, T), w(k, jj * 128), actv(k0 + k), k0 + k == 0, k0 + k == 21)
                            pump(1)
                    fb[cg] = bk
                return fb[cg][j]
            post_norm_residual(l, b0, nb, PG_FFNPOST, prod_ffn)
            for c in range(8):
                dump(f'h2_{c}', H, c * NTOK + HB * 128, c * NTOK + HB * 128 + 512, dz)

        def ple(l, gl, b0, nb, is_last_layer):
            T = nb * 128
            dz = (l == 0 and b0 == 0)
            meng = "POOL" if l >= POOL_FROM_LAYER else "DVE"
            hb = lambda c: G.v(c * 512, c * 512 + T)
            for c in range(8):
                act(hb(c), hv(c, b0, nb), AF.Identity)
            pbv = lambda k: AB.v(11264 + k * 512, 11264 + k * 512 + T)
            for i in range(2):
                w = wtile(l, f"PLEG_{i}")
                for j in range(4):
                    c = 4 * i + j
                    b = fm_group(w, 8, j, lambda k: hb(k), T)
                    act(M.v(c * 512, c * 512 + T), b.v(0, T), AF.Sigmoid)
            w = wtile(l, "PLEW")
            for c in range(8):
                b = bank()
                for k in range(2):
                    mm(b.v(0, T), w(k, c * 128), pbv(k), k == 0, k == 1)
                t_ = tmp(T)
                tt(t_, b.v(0, T), M.v(c * 512, c * 512 + T), ALU.mult)
                tt(hv(c, b0, nb), hv(c, b0, nb), t_, ALU.add, eng=meng)
            for c in range(8):
                dump(f'h3_{c}', H, c * NTOK + HB * 128, c * NTOK + HB * 128 + 512, dz)
            if is_last_layer and b0 >= 0:
                for c in range(8):
                    hvv = hv(c, b0, nb)
                    P.dma("SP", f"out{c % 4}", (lambda e, hvv=hvv, c=c: e.dma_start(out=outT[c][:, b0 * 128:b0 * 128 + T], in_=hvv.ap)),
                          reads=[hvv])

        for l in range(nl):
            sgu_setup(l)
            pb = -(nl - l)
            halo = [(b, 1) for b in range(pb + 1, 0)]
            own = [(b0, 4) for b0 in range(0, NOWN, 4)]
            if halo and len(halo) == 1 and MERGE_HALO:
                sizes = (4, 4, 3, 3, 3)
                tiles_, b_ = [], -1
                for sz in sizes:
                    tiles_.append((b_, sz))
                    b_ += sz
                steps = [("P", pb)] + [("T", t) for t in tiles_]
            else:
                steps = [("P", -1)] + [("T", t) for t in own]
                if halo:
                    steps += [("P", pb)] + [("T", t) for t in halo]
            tl = [s_ for s_ in steps]
            assert tl[0][0] == "P"
            partial_block(l, tl[0][1])
            idx = 1
            pre_rms(l, *tl[idx][1])
            pre_b1(l, *tl[idx][1])
            start_conv(l, *tl[idx][1])
            uv_done = False
            while idx < len(tl):
                b0, nb = tl[idx][1]
                nxt = idx + 1
                has_partial = nxt < len(tl) and tl[nxt][0] == "P"
                nxt_tile = nxt + 1 if has_partial else nxt
                have_next = nxt_tile < len(tl)
                hoist = have_next and not has_partial
                if not uv_done:
                    mixer_uv(l, b0, nb)
                hook = (lambda t=tl[nxt_tile][1]: pre_rms(l, *t)) if hoist else None
                mixer_rest(l, b0, nb, hook)
                if has_partial:
                    partial_block(l, tl[nxt][1])
                if have_next:
                    if not hoist:
                        pre_rms(l, *tl[nxt_tile][1])
                    pre_b1(l, *tl[nxt_tile][1])
                    start_conv(l, *tl[nxt_tile][1])
                ffn(l, b0, nb)
                uv_done = False
                if hoist:
                    mixer_uv(l, *tl[nxt_tile][1])
                    uv_done = True
                ple(l, l, b0, nb, last and l == nl - 1)
                idx = nxt_tile
        print("[kernel] sbuf bytes remaining", nc.sbuf_bytes_remaining)
        import os as _os2
        if _os2.environ.get('KERNEL_NOSCHED') is None:
            est = P.schedule()
            print(f"[kernel] list-scheduled, model makespan {est:.0f} us")
        nops, nwait = P.finalize(st)
        import os as _os
        if _os.environ.get('KERNEL_TAGS'):
            import json as _json
            _json.dump([o.tag for o in P.ops if o.eng == 'PE'], open(_os.environ['KERNEL_TAGS'], 'w'))
        print(f"[kernel] ops={nops} waits={nwait}")
    return nc


def _fm(a):
    nt, nf = a.shape
    return np.ascontiguousarray(a.T.reshape(nf // 128, 128, nt))


def _pool_mats(start):
    out = np.zeros((4, 4, 128, 128), np.float32)
    tp = np.arange(128)[:, None]
    t = np.arange(128)[None, :]
    for g, w in enumerate((2, 4, 8, 16)):
        d = t - tp
        cur = ((d >= 0) & (d < w)).astype(np.float32) / w - (d == 0)
        dp = t + 128 - tp
        prev = ((dp >= 0) & (dp < w)).astype(np.float32) / w
        out[0, g] = cur
        out[1, g] = prev
        if start:
            cnt = np.minimum(np.arange(128) + 1, w).astype(np.float32)[None, :]
            out[2, g] = ((d >= 0) & (d < w)).astype(np.float32) / cnt - (d == 0)
            out[3, g] = 0.0
        else:
            out[2, g] = cur
            out[3, g] = prev
    return np.ascontiguousarray(out.transpose(2, 0, 1, 3).reshape(128, 16 * 128))


_NC_CACHE = {}


def _get_nc(nl):
    if nl not in _NC_CACHE:
        _NC_CACHE[nl] = build(nl, True)
    return _NC_CACHE[nl]


def _param_blob(inp, l):
    cols = []
    for n in ("g_mix_pre", "g_sgu_v", "b_sgu_v", "b_dw", "g_conv_ln", "b_conv_ln", "s_pool", "g_mix_post",
              "g_ffn_pre", "g_ffn_post"):
        cols.append(np.asarray(inp[n][l], np.float32).reshape(8, 128).T)
    wdw = np.asarray(inp["w_dw"][l], np.float32).reshape(31, 8, 128)
    cols.append(wdw.transpose(2, 1, 0).reshape(128, 8 * 31))
    return np.ascontiguousarray(np.concatenate(cols, axis=1))


def make_in_maps(inp, nl):
    HB = nl
    x = np.asarray(inp["x"], np.float32)
    p = np.asarray(inp["p"], np.float32)
    B, S, _ = x.shape
    nq = 8 // B
    SQ = S // nq
    i = np.arange(128)
    ch = i // 64
    maskT = (ch[:, None] <= ch[None, :]).astype(np.float32)
    shared = {wn: np.ascontiguousarray(np.asarray(inp[wn], np.float32).reshape([nl] + WSHAPES[wn])) for wn in WSHAPES}
    shared["pblob"] = np.stack([_param_blob(inp, l) for l in range(nl)])
    ws = np.asarray(inp["w_sgu_s"], np.float32)
    shared["wsT"] = np.ascontiguousarray(ws.transpose(0, 3, 1, 2).reshape(nl, 128, 1024))
    bs = np.asarray(inp["b_sgu_s"], np.float32).reshape(nl, 1, 1024)
    shared["bsb"] = np.ascontiguousarray(np.broadcast_to(bs, (nl, 128, 1024)))
    shared["maskT"] = maskT
    in_maps = []
    for core in range(8):
        b, q = core // nq, core % nq
        s0 = q * SQ
        lo = s0 - HB * 128
        xs = np.zeros((HB * 128 + SQ, D), np.float32)
        ps = np.zeros((nl, HB * 128 + SQ, 256), np.float32)
        if lo >= 0:
            xs[:] = x[b, lo:s0 + SQ]
            ps[:] = p[:, b, lo:s0 + SQ]
        else:
            xs[HB * 128:] = x[b, s0:s0 + SQ]
            ps[:, HB * 128:] = p[:, b, s0:s0 + SQ]
        m = dict(shared)
        m["xT"] = _fm(xs)
        m["pT"] = np.stack([_fm(ps[l]) for l in range(nl)])
        m["pmats"] = _pool_mats(q == 0)
        m["cmask"] = np.full((128, 1), 0.0 if q == 0 else 1.0, np.float32)
        in_maps.append(m)
    return in_maps, (B, S, nq, SQ)


def kernel(**inp):
    nl = 2
    in_maps, (B, S, nq, SQ) = make_in_maps(inp, nl)
    nc = _get_nc(nl)
    res = run_bass_kernel_spmd(nc, in_maps, core_ids=list(range(8)))
    out = np.zeros((B, S, D), np.float32)
    for core in range(8):
        b, q = core // nq, core % nq
        o = res.results[core]["outT"]
        out[b, q * SQ:(q + 1) * SQ] = o.reshape(D, SQ).T
    return out
```

```python
import numpy as np
from contextlib import ExitStack
import concourse.bass as bass
import concourse.mybir as mybir
from concourse.bass_utils import run_bass_kernel_spmd

F32, BF16 = mybir.dt.float32, mybir.dt.bfloat16
AF = mybir.ActivationFunctionType
ALU = mybir.AluOpType

D = 1024
DFF = 2816
NOWN = 16
EPS = 1e-6
NSLOT = 3
RAW, WAW, WAR = 1, 2, 4
STRICT_SAME_ENGINE = True
MERGE_HALO = True
POOL_FROM_LAYER = 99
NPB = 80 + 8 * 31


ACT_TBL = {AF.Gelu_apprx_tanh: 1, AF.Sigmoid: 2, AF.Silu: 3, AF.Ln: 4, AF.Exp: 4}


class V:
    __slots__ = ("ap", "reg", "n")

    def __init__(self, ap, reg, n=512):
        self.ap = ap
        self.reg = reg
        self.n = n


class Buf:
    def __init__(self, name, t, es, cell=256):
        self.name, self.t, self.es, self.cell = name, t, es, cell

    def v(self, lo, hi, a=None):
        ap = self.t[:, lo:hi]
        if a:
            ap = ap.rearrange("p (a b) -> p a b", a=a)
        return V(ap, [(self.name, lo * self.es, hi * self.es, self.cell)], hi - lo)


class Op:
    __slots__ = ("eng", "fn", "reads", "writes", "dma", "chan", "deps", "sig", "need", "clock", "tag", "tbl")


TAG = [""]


class Prog:
    def __init__(self, nc):
        self.nc = nc
        self.ops = []

    def op(self, eng, fn, reads=(), writes=()):
        o = Op()
        o.eng, o.fn, o.reads, o.writes = eng, fn, list(reads), list(writes)
        o.dma, o.chan, o.need, o.sig, o.clock = False, None, False, None, None
        o.tag = TAG[0]
        o.tbl = None
        self.ops.append(o)
        return o

    def dma(self, eng, chan, fn, reads=(), writes=()):
        o = self.op(eng, fn, reads, writes)
        o.dma, o.chan = True, chan
        return o

    def schedule(self):
        import heapq
        ops = self.ops
        n = len(ops)
        cells = {}
        preds = [None] * n
        for i, X in enumerate(ops):
            deps = set()
            for v in X.reads:
                for (name, lo, hi, cell) in v.reg:
                    d = cells.setdefault(name, {})
                    for c in range(lo // cell, (hi - 1) // cell + 1):
                        st = d.get(c)
                        if st is not None and st[0] is not None:
                            deps.add(st[0])
            for v in X.writes:
                for (name, lo, hi, cell) in v.reg:
                    d = cells.setdefault(name, {})
                    for c in range(lo // cell, (hi - 1) // cell + 1):
                        st = d.get(c)
                        if st is not None:
                            if st[0] is not None:
                                deps.add(st[0])
                            deps.update(st[1])
            for v in X.reads:
                for (name, lo, hi, cell) in v.reg:
                    d = cells[name]
                    for c in range(lo // cell, (hi - 1) // cell + 1):
                        st = d.get(c)
                        if st is None:
                            d[c] = [None, {i}]
                        else:
                            st[1].add(i)
            for v in X.writes:
                for (name, lo, hi, cell) in v.reg:
                    d = cells[name]
                    for c in range(lo // cell, (hi - 1) // cell + 1):
                        d[c] = [i, set()]
            deps.discard(i)
            preds[i] = deps
        cost = [0.0] * n
        lat = [0.0] * n
        for i, X in enumerate(ops):
            nn = X.writes[0].n if X.writes else 512
            if X.dma:
                nb = 0
                for v in X.writes:
                    for (name, lo, hi, cell) in v.reg:
                        nb += (hi - lo) * (128 if cell != 1 else 8192 * 128)
                cost[i] = 0.08
                lat[i] = 2.0 + nb / 200e3
            elif X.eng == "PE":
                cost[i] = max(0.055, nn * 0.22 / 512)
            elif X.eng == "ACT":
                cost[i] = 0.19 + nn * 0.00083
            else:
                cost[i] = 0.16 + nn * 0.00104
        succs = [[] for _ in range(n)]
        indeg = [0] * n
        for i in range(n):
            indeg[i] = len(preds[i])
            for p in preds[i]:
                succs[p].append(i)
        bl = [0.0] * n
        for i in range(n - 1, -1, -1):
            m = 0.0
            for s_ in succs[i]:
                if bl[s_] > m:
                    m = bl[s_]
            bl[i] = m + cost[i] + lat[i]
        ready_t = [0.0] * n
        engs = ("PE", "ACT", "DVE", "POOL", "SP")
        rq = {e: [] for e in engs}
        free_t = {e: 0.0 for e in engs}
        for i in range(n):
            if indeg[i] == 0:
                heapq.heappush(rq[ops[i].eng], (-bl[i], i))
        start = [0.0] * n
        done = 0
        cur_tbl = [None]
        SEM = 0.3
        while done < n:
            best = None
            for e in engs:
                q = rq[e]
                if not q:
                    continue
                cand = None
                tfree = free_t[e]
                top = heapq.nsmallest(6, q)
                if e == "ACT":
                    for (nb_, i) in top:
                        if ready_t[i] <= tfree and (ops[i].tbl is None or ops[i].tbl == cur_tbl[0]):
                            cand = (tfree, nb_, i)
                            break
                if cand is None:
                    for (nb_, i) in top:
                        if ready_t[i] <= tfree:
                            cand = (tfree, nb_, i)
                            break
                if cand is None:
                    (nb_, i) = min(top, key=lambda x: (ready_t[x[1]], x[0]))
                    cand = (ready_t[i], nb_, i)
                if best is None or cand[0] < best[0][0]:
                    best = (cand, e)
            (t0, nb_, i), e = best
            rq[e].remove((nb_, i))
            heapq.heapify(rq[e])
            start[i] = t0
            fin = t0 + cost[i]
            if e == "ACT" and ops[i].tbl is not None:
                if ops[i].tbl != cur_tbl[0]:
                    fin += 1.3
                    cur_tbl[0] = ops[i].tbl
            free_t[e] = fin
            fin += lat[i]
            for s_ in succs[i]:
                if ready_t[s_] < fin + SEM:
                    ready_t[s_] = fin + SEM
                indeg[s_] -= 1
                if indeg[s_] == 0:
                    heapq.heappush(rq[ops[s_].eng], (-bl[s_], s_))
            done += 1
        order = sorted(range(n), key=lambda i: (start[i], i))
        self.ops = [ops[i] for i in order]
        return max(free_t.values())

    def finalize(self, stack):
        nc = self.nc
        ops = self.ops
        cells = {}
        for i, X in enumerate(ops):
            deps = {}
            for v in X.reads:
                for (name, lo, hi, cell) in v.reg:
                    d = cells.setdefault(name, {})
                    for c in range(lo // cell, (hi - 1) // cell + 1):
                        st = d.get(c)
                        if st is not None and st[0] is not None:
                            deps[st[0]] = deps.get(st[0], 0) | RAW
            for v in X.writes:
                for (name, lo, hi, cell) in v.reg:
                    d = cells.setdefault(name, {})
                    for c in range(lo // cell, (hi - 1) // cell + 1):
                        st = d.get(c)
                        if st is not None:
                            if st[0] is not None:
                                deps[st[0]] = deps.get(st[0], 0) | WAW
                            for r in st[1].values():
                                deps[r] = deps.get(r, 0) | WAR
            for v in X.reads:
                for (name, lo, hi, cell) in v.reg:
                    d = cells[name]
                    for c in range(lo // cell, (hi - 1) // cell + 1):
                        st = d.get(c)
                        if st is None:
                            d[c] = [None, {X.eng: i}]
                        else:
                            st[1][X.eng] = i
            for v in X.writes:
                for (name, lo, hi, cell) in v.reg:
                    d = cells[name]
                    for c in range(lo // cell, (hi - 1) // cell + 1):
                        d[c] = [i, {}]
            deps.pop(i, None)
            fd = {}
            for p, kind in deps.items():
                Pp = ops[p]
                if not Pp.dma and Pp.eng == X.eng and not X.dma:
                    if X.eng == "PE" or not (STRICT_SAME_ENGINE or (kind & RAW)):
                        continue
                if not Pp.dma and Pp.eng == X.eng and X.dma and not (kind & RAW):
                    continue
                fd[p] = kind
                Pp.need = True
            X.deps = fd
        sems = {}

        def sem(k):
            if k not in sems:
                sems[k] = stack.enter_context(nc.semaphore("s_" + k))
            return sems[k]

        engobj = {"PE": nc.tensor, "ACT": nc.scalar, "DVE": nc.vector, "POOL": nc.gpsimd, "SP": nc.sync}
        know = {k: {} for k in engobj}
        cnt = {}
        lastchan = {}
        nwait = 0
        for X in ops:
            F = X.eng
            K = know[F]
            waits = {}
            for p in X.deps:
                Pp = ops[p]
                k, v = Pp.sig
                if K.get(k, 0) < v:
                    waits[k] = max(waits.get(k, 0), v)
                for kk, vv in Pp.clock.items():
                    if K.get(kk, 0) < vv:
                        K[kk] = vv
            if X.dma:
                pv = lastchan.get(X.chan, 0)
                if K.get(X.chan, 0) < pv:
                    waits[X.chan] = max(waits.get(X.chan, 0), pv)
                    K[X.chan] = pv
                lastchan[X.chan] = pv + 16
                X.sig = (X.chan, pv + 16)
            elif X.need:
                cnt[F] = cnt.get(F, 0) + 1
                X.sig = (F, cnt[F])
            X.clock = dict(K)
            if X.sig:
                X.clock[X.sig[0]] = X.sig[1]
            e = engobj[F]
            wl = list(waits.items())
            nwait += len(wl)
            attach = None
            if wl and not X.dma and X.eng != "PE":
                attach = wl.pop()
            for (k, v) in wl:
                e.wait_ge(sem(k), v)
            ins = X.fn(e)
            if attach is not None:
                ins._wait_ge(sem(attach[0]), attach[1])
            if X.sig:
                ins.then_inc(sem(X.sig[0]), 16 if X.dma else 1)
        K = know["SP"]
        for ch, v in lastchan.items():
            nc.sync.wait_ge(sem(ch), v)
        return len(ops), nwait


def tile_specs():
    sp = {}
    order = []

    def add(name, kc, ncc, segs):
        sp[name] = (kc, ncc, segs)
        order.append(name)

    for i in range(4):
        add(f"B1_{i}", 8, 512, [("w_in", 0, 2048 + 256 * i, 256, 0), ("w_in", 0, 3072 + 256 * i, 256, 256)])
    for i in range(2):
        add(f"ZP_{i}", 8, 512, [("w_in", 0, 4096 + 512 * i, 512, 0)])
    for i in range(2):
        add(f"U_{i}", 8, 512, [("w_in", 0, 512 * i, 512, 0)])
    for i in range(2):
        add(f"V_{i}", 8, 512, [("w_in", 0, 1024 + 512 * i, 512, 0)])
    for i in range(2):
        add(f"G0_{i}", 8, 512, [("w_in", 0, 5120 + 512 * i, 512, 0)])
    for i in range(2):
        add(f"SGUOUT_{i}", 8, 512, [("w_sgu_out", 0, 512 * i, 512, 0)])
    add("WPOOL", 2, 1024, [("w_pool", g, 0, 256, 256 * g) for g in range(4)])
    for i in range(2):
        add(f"G2_{i}", 8, 512, [("w_in", 0, 7168 + 512 * i, 512, 0)])
    for i in range(2):
        add(f"POOLOUT_{i}", 8, 512, [("w_pool_out", 0, 512 * i, 512, 0)])
    for i in range(2):
        add(f"G1_{i}", 8, 512, [("w_in", 0, 6144 + 512 * i, 512, 0)])
    for i in range(2):
        add(f"CONVOUT_{i}", 8, 512, [("w_conv_out", 0, 512 * i, 512, 0)])
    for i in range(2):
        add(f"WOUT_{i}", 8, 512, [("w_out", 0, 512 * i, 512, 0)])
    for i in range(11):
        add(f"FFNIN_{i}", 8, 512, [("w_ffn_in", 0, 256 * i, 256, 0), ("w_ffn_in", 0, DFF + 256 * i, 256, 256)])
    for cg in range(2):
        for kg, (k0, kc) in enumerate(((0, 8), (8, 8), (16, 6))):
            add(f"FFNOUT_{cg}_{kg}", kc, 512, [("w_ffn_out", k0 * 128, 512 * cg, 512, 0)])
    for i in range(2):
        add(f"PLEG_{i}", 8, 512, [("w_ple_gate", 0, 512 * i, 512, 0)])
    add("PLEW", 2, 1024, [("w_ple", 0, 0, 1024, 0)])
    return sp, order


WSHAPES = {
    "w_in": [D, 8192], "w_sgu_out": [D, D], "w_conv_out": [D, D], "w_pool": [4, 256, 256],
    "w_pool_out": [D, D], "w_out": [D, D], "w_ffn_in": [D, 2 * DFF], "w_ffn_out": [DFF, D],
    "w_ple": [256, D], "w_ple_gate": [D, D],
}


def build(nl, last, dbg=False):
    HB = nl
    NBT = HB + NOWN
    NTOK = NBT * 128
    nc = bass.Bass("TRN2", target_bir_lowering=False)
    P = Prog(nc)
    specs, order = tile_specs()
    NTL = len(order)
    tidx = {(l, n): l * NTL + i for l in range(nl) for i, n in enumerate(order)}

    din = {}

    def dram_in(name, shape):
        din[name] = nc.dram_tensor(name, shape, F32, kind="ExternalInput").ap()
        return din[name]

    xT = dram_in("xT", [8, 128, NTOK])
    pT = dram_in("pT", [nl, 2, 128, NTOK])
    for wn, shp in WSHAPES.items():
        dram_in(wn, [nl] + shp)
    pblob_d = dram_in("pblob", [nl, 128, NPB])
    wsT_d = dram_in("wsT", [nl, 128, 1024])
    bsb_d = dram_in("bsb", [nl, 128, 1024])
    maskT_d = dram_in("maskT", [128, 128])
    pmats_d = dram_in("pmats", [128, 16 * 128])
    cmask_d = dram_in("cmask", [128, 1])
    outT = nc.dram_tensor("outT", [8, 128, NOWN * 128], F32, kind="ExternalOutput").ap()
    wscr = nc.dram_tensor("wscr", [nl * NTL, 128, 4096], BF16).ap()

    with ExitStack() as st:
        def sb(name, n, dt, cell=256):
            t = st.enter_context(nc.sbuf_tensor(name, [128, n], dt))
            return Buf(name, t, 4 if dt == F32 else 2, cell)

        H = sb("H", 8 * NTOK, F32)
        WS = sb("WS", NSLOT * 4096, BF16)
        HN = sb("HN", 8 * 512, BF16)
        AB = sb("AB", 12 * 1024, BF16)
        ZPV = sb("ZPV", 1024, BF16)
        G = sb("G", 8 * 512, BF16)
        M = sb("M", 8 * 512, F32)
        GLU = sb("GLU", 8 * 542, F32, 64)
        NH = sb("NH", 8 * 30, F32, 64)
        TMP = sb("TMP", 3 * 512, F32)
        SQB = sb("SQB", 2 * 512, BF16)
        CACC = sb("CACC", 2 * 512, F32)
        WST = sb("WST", 1024, BF16)
        CC = sb("CC", 1024, F32)
        PB = sb("PB", nl * NPB, F32, 64)
        PM = sb("PM", 16 * 128, BF16)
        PMF = None
        MASK = sb("MASK", 128, F32)
        ONES = sb("ONES", 128, BF16)
        SM = sb("SM", 128, F32, 4)
        banks = []
        for i in range(8):
            t = st.enter_context(nc.psum_tensor(f"ps{i}", [128, 512], F32))
            banks.append(Buf(f"ps{i}", t, 4, 2048))
        bank_i = [0]

        def bank():
            b = banks[bank_i[0] % 6]
            bank_i[0] += 1
            return b

        tmp_i = [0]

        def tmp(T):
            i = tmp_i[0] % 3
            tmp_i[0] += 1
            return TMP.v(i * 512, i * 512 + T)

        sq_i = [0]

        def sqb(T):
            i = sq_i[0] % 2
            sq_i[0] += 1
            return SQB.v(i * 512, i * 512 + T)

        ABF = Buf("AB", AB.t, 2, 256)

        def ab_bf(lo_b, n):
            return AB.v(lo_b // 2, lo_b // 2 + n)

        ABf32_t = AB.t

        VG = sb("VG", 2 * 1024, F32)
        PF = VG

        class Sub:
            def __init__(self, buf, off):
                self.buf, self.off, self.t = buf, off, buf.t

            def v(self, lo, hi):
                return self.buf.v(self.off + lo, self.off + hi)
        MEAN = Sub(VG, 0)
        RSTD = Sub(VG, 512)
        PUMPN = [1]
        bgq = []
        pump_ctr = [0]

        def pump(n):
            for _ in range(n * PUMPN[0]):
                for ent in list(bgq):
                    g, rate = ent
                    for _r in range(rate):
                        try:
                            next(g)
                        except StopIteration:
                            bgq.remove(ent)
                            break

        def flush():
            while bgq:
                pump(1)

        def rd(x):
            return [x] if isinstance(x, V) else []

        def apf(x):
            return x.ap if isinstance(x, V) else x

        def act(out, in_, func, bias=None, scale=None):
            kw = {}
            if bias is not None:
                kw["bias"] = apf(bias)
            if scale is not None:
                kw["scale"] = apf(scale)
            o = P.op("ACT", lambda e: e.activation(out=out.ap, in_=in_.ap, func=func, **kw),
                     reads=[in_] + rd(bias) + rd(scale), writes=[out])
            o.tbl = ACT_TBL.get(func)

        def tt(out, a, b, op, eng="DVE"):
            P.op(eng, lambda e: e.tensor_tensor(out=out.ap, in0=a.ap, in1=b.ap, op=op), reads=[a, b], writes=[out])

        def stt(out, in0, scalar, in1, op0, op1, eng="DVE"):
            P.op(eng, lambda e: e.scalar_tensor_tensor(out=out.ap, in0=in0.ap, scalar=apf(scalar), in1=in1.ap,
                                                       op0=op0, op1=op1),
                 reads=[in0, in1] + rd(scalar), writes=[out])

        def ts(out, in0, s1, s2, op0, op1=None, eng="DVE"):
            if op1 is None:
                P.op(eng, lambda e: e.tensor_scalar(out=out.ap, in0=in0.ap, scalar1=apf(s1), scalar2=None, op0=op0),
                     reads=[in0] + rd(s1), writes=[out])
            else:
                P.op(eng, lambda e: e.tensor_scalar(out=out.ap, in0=in0.ap, scalar1=apf(s1), scalar2=apf(s2),
                                                    op0=op0, op1=op1),
                     reads=[in0] + rd(s1) + rd(s2), writes=[out])

        def cp(out, in_, eng="DVE"):
            P.op(eng, lambda e: e.tensor_copy(out=out.ap, in_=in_.ap), reads=[in_], writes=[out])

        def recip(out, in_):
            P.op("DVE", lambda e: e.reciprocal(out=out.ap, in_=in_.ap), reads=[in_], writes=[out])

        def mm(out, lhsT, rhs, start, stop):
            P.op("PE", lambda e: e.matmul(out.ap, lhsT=lhsT.ap, rhs=rhs.ap, start=start, stop=stop),
                 reads=[lhsT, rhs], writes=[out])

        def load(chan, out, src_ap, src_reg=None):
            P.dma("SP", chan, lambda e: e.dma_start(out=out.ap, in_=src_ap),
                  reads=[V(None, src_reg)] if src_reg else [], writes=[out])

        def pcol(l, j, n=1):
            return PB.v(l * NPB + j, l * NPB + j + n)

        PG_MIXPRE, PG_SGUV, PB_SGUV, PB_DW, PG_CLN, PB_CLN, PS_POOL, PG_MIXPOST, PG_FFNPRE, PG_FFNPOST, PW_DW = (
            0, 8, 16, 24, 32, 40, 48, 56, 64, 72, 80)
        EPSV = SM.v(0, 1)
        CMASK = SM.v(1, 2)

        P.op("DVE", lambda e: e.memset(ONES.t[:, :], 1.0), writes=[ONES.v(0, 128)])
        P.op("DVE", lambda e: e.memset(SM.t[:, 0:1], EPS), writes=[EPSV])
        load("misc", CMASK, cmask_d[:, :])
        for l in range(nl):
            load("misc", PB.v(l * NPB, (l + 1) * NPB), pblob_d[l])
        load("misc", MASK.v(0, 128), maskT_d[:, :])
        for hh in range(2):
            load("misc", VG.v(1024, 2048), pmats_d[:, hh * 1024:(hh + 1) * 1024])
            cp(PM.v(hh * 1024, (hh + 1) * 1024), VG.v(1024, 2048))
        xsplit = (HB + 3) * 128
        for c in range(8):
            load(f"x{c % 4}", H.v(c * NTOK, c * NTOK + xsplit), xT[c][:, 0:xsplit])
        for c in range(8):
            load(f"x{c % 4}", H.v(c * NTOK + xsplit, (c + 1) * NTOK), xT[c][:, xsplit:NTOK])
        ci = 0
        for l in range(nl):
            for n in order:
                kc, ncc, segs = specs[n]
                ti = tidx[(l, n)]
                dst3 = wscr[ti][:, 0:kc * ncc].rearrange("p (k c) -> p k c", k=kc)
                for si, (wn, r0, c0, ncols, doff) in enumerate(segs):
                    if wn == "w_pool":
                        src = din[wn][l, r0, :, c0:c0 + ncols]
                    else:
                        src = din[wn][l, r0:r0 + kc * 128, c0:c0 + ncols]
                    src3 = src.rearrange("(k p) c -> p k c", p=128)
                    d3 = dst3[:, :, doff:doff + ncols]
                    P.dma("POOL", f"cast{ci % 6}", (lambda e, d3=d3, src3=src3: e.dma_start(out=d3, in_=src3)),
                          writes=[V(None, [("wscr", ti * 4 + si, ti * 4 + si + 1, 1)])])
                    ci += 1

        def dump(name, buf, lo, hi, on):
            if not (dbg and on):
                return
            dt = BF16 if buf.es == 2 else F32
            d = nc.dram_tensor("dbg_" + name, [128, hi - lo], dt, kind="ExternalOutput").ap()
            v = buf.v(lo, hi)
            P.dma("SP", "dbg", (lambda e, d=d, v=v: e.dma_start(out=d[:, :], in_=v.ap)), reads=[v])

        slot_i = [0]

        def wtile(l, name):
            kc, ncc, segs = specs[name]
            TAG[0] = name.split('_')[0]
            ti = tidx[(l, name)]
            s = slot_i[0] % NSLOT
            slot_i[0] += 1
            dst = WS.v(s * 4096, s * 4096 + kc * ncc)
            load(f"w{s}", dst, wscr[ti][:, 0:kc * ncc], [("wscr", ti * 4, ti * 4 + 4, 1)])

            def w(k, c0, n=128):
                o = s * 4096 + k * ncc + c0
                return WS.v(o, o + n)
            return w

        def hv(c, b0, nb):
            o = c * NTOK + (b0 + HB) * 128
            return H.v(o, o + nb * 128)

        def hn(c, T, t0=0):
            return HN.v(c * 512 + t0, c * 512 + T)

        def rstd_of(out, in_, scale):
            act(out, in_, AF.Ln, bias=EPSV, scale=scale)
            act(out, out, AF.Exp, scale=-0.5)

        def rstd_from_bank(S, T):
            rstd_of(RSTD.v(0, T), S.v(0, T), 1.0 / D)

        def rms_to_hn(l, b0, nb, gcol, dst=None):
            dst = dst or HN
            T = nb * 128
            TAG[0] = 'rms'
            S = banks[7]
            for c in range(8):
                q = sqb(T)
                act(q, hv(c, b0, nb), AF.Square)
                mm(S.v(0, T), ONES.v(0, 128), q, c == 0, c == 7)
            rstd_from_bank(S, T)
            for c in range(8):
                stt(dst.v(c * 512, c * 512 + T), hv(c, b0, nb), pcol(l, gcol + c), RSTD.v(0, T), ALU.mult, ALU.mult)

        def fm_group(w, kc, oc, rhs_fn, T):
            b = bank()
            for k in range(kc):
                mm(b.v(0, T), w(k, oc * 128), rhs_fn(k), k == 0, k == kc - 1)
            pump(1)
            return b

        def b1_tiles(l, T, dst_fn):
            for i in range(4):
                w = wtile(l, f"B1_{i}")
                for j in range(2):
                    c = 2 * i + j
                    gb = fm_group(w, 8, 2 + j, lambda k: hn(k, T), T)
                    sg = tmp(T)
                    act(sg, gb.v(0, T), AF.Sigmoid)
                    ab_ = fm_group(w, 8, j, lambda k: hn(k, T), T)
                    dst_fn(c, ab_, sg)

        def zp_tiles(l, nb, slot0):
            T = nb * 128
            for i in range(2):
                w = wtile(l, f"ZP_{i}")
                for b in range(nb):
                    pb_ = bank()
                    for k in range(8):
                        mm(pb_.v(0, 512), hn(k, b * 128 + 128, b * 128), w(k, 0, 512), k == 0, k == 7)
                    o = zp_off(slot0 + b) + i * 512
                    act(AB.v(o, o + 512), pb_.v(0, 512), AF.Identity)
                    pump(1)

        def zp_off(slot):
            return 4096 + (slot - 1) * 1024

        def zp_view(slot, c):
            if slot == 0:
                return ZPV.v(c * 128, c * 128 + 128)
            o = zp_off(slot) + c * 128
            return AB.v(o, o + 128)

        def sgu_setup(l):
            load("misc", VG.v(1024, 2048), wsT_d[l])
            mask_bc = V(bass.AP(MASK.t, 0, [[128, 128], [0, 8], [1, 128]]), MASK.v(0, 128).reg)
            P.op("DVE", lambda e: e.tensor_tensor(out=WST.t[:, :].rearrange("p (h i) -> p h i", h=8),
                                                  in0=VG.t[:, 1024:2048].rearrange("p (h i) -> p h i", h=8),
                                                  in1=mask_bc.ap, op=ALU.mult),
                 reads=[VG.v(1024, 2048), mask_bc], writes=[WST.v(0, 1024)])
            rb = [bank(), bank()]
            for i in range(2):
                mm(rb[i].v(0, 512), ONES.v(0, 128), WST.v(i * 512, i * 512 + 512), True, True)
            load("misc", VG.v(1024, 2048), bsb_d[l])
            for h in range(8):
                stt(CC.v(h * 128, h * 128 + 128), rb[h // 4].v((h % 4) * 128, (h % 4) * 128 + 128),
                    pcol(l, PB_SGUV + h), VG.v(1024 + h * 128, 1024 + h * 128 + 128), ALU.mult, ALU.add)

        state = {"zlast": 4}

        def partial_block(l, pb):
            rms_to_hn(l, pb, 1, PG_MIXPRE)

            def dst(c, a_bank, sg):
                tt(NH.v(c * 30, c * 30 + 30), a_bank.v(98, 128), V(sg.ap[:, 98:128], sg.reg), ALU.mult)
            b1_tiles(l, 128, dst)
            zp_tiles(l, 1, 4)
            state["zlast"] = 4

        def post_norm_residual(l, b0, nb, gcol, producer):
            T = nb * 128
            S = banks[6]
            pend = []
            for c in range(8):
                yb_ = producer(c)
                q = sqb(T)
                act(q, yb_.v(0, T), AF.Square)
                act(M.v(c * 512, c * 512 + T), yb_.v(0, T), AF.Identity)
                pend.append((c, q))
                if len(pend) > 1:
                    cc_, qq = pend.pop(0)
                    mm(S.v(0, T), ONES.v(0, 128), qq, cc_ == 0, False)
            cc_, qq = pend.pop(0)
            mm(S.v(0, T), ONES.v(0, 128), qq, False, True)
            rstd_from_bank(S, T)
            for c in range(8):
                y = M.v(c * 512, c * 512 + T)
                tt(y, y, RSTD.v(0, T), ALU.mult)
                stt(hv(c, b0, nb), y, pcol(l, gcol + c), hv(c, b0, nb), ALU.mult, ALU.add)

        def pre_rms(l, b0, nb):
            rms_to_hn(l, b0, nb, PG_MIXPRE)

        def pre_b1(l, b0, nb):
            T = nb * 128
            nh3 = V(NH.t[:, :].rearrange("p (c n) -> p c n", c=8), NH.v(0, 240).reg)
            gl3 = V(GLU.t[:, :].rearrange("p (c n) -> p c n", c=8)[:, :, 0:30], GLU.v(0, 8 * 542).reg)
            if b0 == 0:
                ts(gl3, nh3, CMASK, None, ALU.mult)
            else:
                cp(gl3, nh3)
            if state["zlast"] != 0:
                zl = state["zlast"]
                cp(ZPV.v(0, 1024), AB.v(zp_off(zl), zp_off(zl) + 1024))
            def glu_dst(c, a_bank, sg):
                tt(GLU.v(c * 542 + 30, c * 542 + 30 + T), a_bank.v(0, T), sg, ALU.mult)
            dz = (l == 0 and b0 == 0)
            dump('hn', HN, 0, 4096, dz)
            b1_tiles(l, T, glu_dst)
            dump('glu', GLU, 0, 8 * 542, dz)

        def conv_gen(l, b0, nb, chunks, eng, slot):
            T = nb * 128
            a = CACC.v(slot * 512, slot * 512 + T)
            pr = CACC.v((slot + 1) * 512, (slot + 1) * 512 + T) if eng != "DVE" else None
            for c in chunks:
                g0 = c * 542
                for k in range(31):
                    wk = pcol(l, PW_DW + c * 31 + k)
                    if k == 0:
                        ts(a, GLU.v(g0, g0 + T), wk, pcol(l, PB_DW + c), ALU.mult, ALU.add, eng=eng)
                    elif k < 30:
                        if eng == "DVE":
                            stt(a, GLU.v(g0 + k, g0 + k + T), wk, a, ALU.mult, ALU.add)
                        else:
                            ts(pr, GLU.v(g0 + k, g0 + k + T), wk, None, ALU.mult, eng=eng)
                            yield
                            tt(a, a, pr, ALU.add, eng=eng)
                    else:
                        cp(NH.v(c * 30, c * 30 + 30), GLU.v(g0 + T, g0 + T + 30), eng=eng)
                        co_ = GLU.v(g0 + 30, g0 + 30 + T)
                        if eng == "DVE":
                            stt(co_, co_, wk, a, ALU.mult, ALU.add)
                        else:
                            ts(pr, co_, wk, None, ALU.mult, eng=eng)
                            yield
                            tt(co_, a, pr, ALU.add, eng=eng)
                    yield

        def start_conv(l, b0, nb):
            assert not bgq
            if l >= POOL_FROM_LAYER:
                bgq.append([conv_gen(l, b0, nb, (0, 1, 2, 3, 4, 5), "DVE", 0), 2])
                bgq.append([conv_gen(l, b0, nb, (6, 7), "POOL", 1), 1])
            else:
                bgq.append([conv_gen(l, b0, nb, (0, 1, 2, 3), "DVE", 0), 1])
                bgq.append([conv_gen(l, b0, nb, (4, 5, 6, 7), "DVE", 1), 1])

        def mixer_uv(l, b0, nb):
            T = nb * 128
            for i in range(2):
                w = wtile(l, f"U_{i}")
                for j in range(4):
                    c = 4 * i + j
                    b = fm_group(w, 8, j, lambda k: hn(k, T), T)
                    act(AB.v(c * 512, c * 512 + T), b.v(0, T), AF.Gelu_apprx_tanh)
            wv = [None, None]
            MV = SM.v(8, 8 + 2 * nb)
            for b in range(nb):
                vgo = (b % 2) * 1024
                for i in range(2):
                    if b == 0:
                        wv[i] = wtile(l, f"V_{i}")
                for i in range(2):
                    pb_ = bank()
                    for k in range(8):
                        mm(pb_.v(0, 512), hn(k, b * 128 + 128, b * 128), wv[i](k, 0, 512), k == 0, k == 7)
                    act(VG.v(vgo + i * 512, vgo + i * 512 + 512), pb_.v(0, 512), AF.Gelu_apprx_tanh)
                    pump(1)
                    bs = SM.v(32 + 6 * i, 38 + 6 * i)
                    P.op("DVE", lambda e, o=bs, s=VG.v(vgo + i * 512, vgo + i * 512 + 512): e.bn_stats(out=o.ap, in_=s.ap),
                         reads=[VG.v(vgo + i * 512, vgo + i * 512 + 512)], writes=[bs])
                mvb = SM.v(8 + 2 * b, 10 + 2 * b)
                P.op("DVE", lambda e, o=mvb, s=SM.v(32, 44): e.bn_aggr(out=o.ap, in_=s.ap), reads=[SM.v(32, 44)], writes=[mvb])
                rs = SM.v(48 + b, 49 + b)
                nm = SM.v(56 + b, 57 + b)
                rstd_of(rs, SM.v(9 + 2 * b, 10 + 2 * b), 1.0)
                stt(nm, SM.v(8 + 2 * b, 9 + 2 * b), -1.0, rs, ALU.mult, ALU.mult)
                vh = AB.v(4096 + b * 1024, 4096 + (b + 1) * 1024)
                act(vh, VG.v(vgo, vgo + 1024), AF.Identity, bias=nm, scale=rs)

        def mixer_rest(l, b0, nb, hook):
            T = nb * 128
            PUMPN[0] = 1
            meng = "POOL" if l >= POOL_FROM_LAYER else "DVE"
            dz = (l == 0 and b0 == 0)
            for i in range(2):
                w = wtile(l, f"G0_{i}")
                for j in range(4):
                    c = 4 * i + j
                    b = fm_group(w, 8, j, lambda k: hn(k, T), T)
                    act(G.v(c * 512, c * 512 + T), b.v(0, T), AF.Sigmoid)
            dump('u', AB, 0, 4096, dz)
            dump('vhat', AB, 4096, 8192, dz)
            dump('g0', G, 0, 4096, dz)
            cc_bc = lambda h: V(bass.AP(CC.t, h * 128, [[1024, 128], [0, nb], [1, 128]]), CC.v(h * 128, h * 128 + 128).reg)
            TAG[0] = 'sgumm'
            for h in range(8):
                mb_ = bank()
                for b in range(nb):
                    vh = AB.v(4096 + b * 1024 + h * 128, 4096 + b * 1024 + h * 128 + 128)
                    mm(mb_.v(b * 128, b * 128 + 128), vh, WST.v(h * 128, h * 128 + 128), True, True)
                t_ = tmp(T)
                cb = cc_bc(h)
                t3 = V(t_.ap.rearrange("p (b i) -> p b i", b=nb), t_.reg)
                m3 = V(mb_.v(0, T).ap.rearrange("p (b i) -> p b i", b=nb), mb_.v(0, T).reg)
                stt(t3, m3, pcol(l, PG_SGUV + h), cb, ALU.mult, ALU.add)
                u = AB.v(h * 512, h * 512 + T)
                tt(u, t_, u, ALU.mult)
            for i in range(2):
                w = wtile(l, f"SGUOUT_{i}")
                for j in range(4):
                    c = 4 * i + j
                    b = fm_group(w, 8, j, lambda k: AB.v(k * 512, k * 512 + T), T)
                    tt(M.v(c * 512, c * 512 + T), b.v(0, T), G.v(c * 512, c * 512 + T), ALU.mult)
            dump('sgo', AB, 0, 4096, dz)
            dump('m_a', M, 0, 4096, dz)
            zp_tiles(l, nb, 1)
            pooled = lambda c: AB.v(c * 512, c * 512 + T)
            TAG[0] = 'poolmm'
            for c in range(8):
                g = c // 2
                pb_ = bank()
                for b in range(nb):
                    first = (b0 + b == 0)
                    kc_, kp_ = (2, 3) if first else (0, 1)
                    pc = PM.v((kc_ * 4 + g) * 128, (kc_ * 4 + g) * 128 + 128)
                    pp = PM.v((kp_ * 4 + g) * 128, (kp_ * 4 + g) * 128 + 128)
                    mm(pb_.v(b * 128, b * 128 + 128), zp_view(b + 1, c), pc, True, False)
                    mm(pb_.v(b * 128, b * 128 + 128), zp_view(b, c), pp, False, True)
                act(pooled(c), pb_.v(0, T), AF.Identity)
                pump(1)
            dump('zp', AB, 4096, 8192, dz)
            dump('pooled', AB, 0, 4096, dz)
            state["zlast"] = nb
            w = wtile(l, "WPOOL")
            plo = lambda c: AB.v(8192 + c * 512, 8192 + c * 512 + T)
            for g in range(4):
                for hh in range(2):
                    pb_ = bank()
                    for k in range(2):
                        mm(pb_.v(0, T), w(k, g * 256 + hh * 128), pooled(2 * g + k), k == 0, k == 1)
                    c = 2 * g + hh
                    act(plo(c), pb_.v(0, T), AF.Identity, scale=pcol(l, PS_POOL + c))
            for i in range(2):
                w = wtile(l, f"G2_{i}")
                for j in range(4):
                    c = 4 * i + j
                    b = fm_group(w, 8, j, lambda k: hn(k, T), T)
                    act(G.v(c * 512, c * 512 + T), b.v(0, T), AF.Sigmoid)
            cp(ZPV.v(0, 1024), AB.v(zp_off(nb), zp_off(nb) + 1024))
            state["zlast"] = 0
            flush()
            dump('co', GLU, 0, 8 * 542, dz)
            S1, S2 = banks[6], banks[7]
            co = lambda c: GLU.v(c * 542 + 30, c * 542 + 30 + T)
            cvo = lambda c: AB.v(c * 512, c * 512 + T)
            mean = MEAN.v(0, T)
            for i in range(2):
                w = wtile(l, f"POOLOUT_{i}")
                for j in range(4):
                    c = 4 * i + j
                    q1 = sqb(T)
                    act(q1, co(c), AF.Identity)
                    q2 = sqb(T)
                    act(q2, co(c), AF.Square)
                    TAG[0] = 'POOLOUT'
                    b = fm_group(w, 8, j, lambda k: plo(k), T)
                    TAG[0] = 'lnstat'
                    mm(S1.v(0, T), ONES.v(0, 128), q1, c == 0, c == 7)
                    mm(S2.v(0, T), ONES.v(0, 128), q2, c == 0, c == 7)
                    t_ = tmp(T)
                    tt(t_, b.v(0, T), G.v(c * 512, c * 512 + T), ALU.mult)
                    tt(M.v(c * 512, c * 512 + T), M.v(c * 512, c * 512 + T), t_, ALU.add, eng=meng)
            dump('plo', AB, 8192, 12288, dz)
            dump('m_c', M, 0, 4096, dz)
            ts(mean, S1.v(0, T), 1.0 / D, None, ALU.mult)
            msq = tmp(T)
            tt(msq, mean, mean, ALU.mult)
            var = tmp(T)
            stt(var, S2.v(0, T), 1.0 / D, msq, ALU.mult, ALU.subtract)
            rstd_of(RSTD.v(0, T), var, 1.0)
            for i in range(2):
                w = wtile(l, f"G1_{i}")
                for j in range(4):
                    c = 4 * i + j
                    b = fm_group(w, 8, j, lambda k: hn(k, T), T)
                    act(G.v(c * 512, c * 512 + T), b.v(0, T), AF.Sigmoid)
                    tt(co(c), co(c), mean, ALU.subtract)
                    tt(co(c), co(c), RSTD.v(0, T), ALU.mult)
                    act(cvo(c), co(c), AF.Silu, bias=pcol(l, PB_CLN + c), scale=pcol(l, PG_CLN + c))
            if hook is not None:
                hook()
            mbv = lambda c: AB.v(4096 + c * 512, 4096 + c * 512 + T)
            for i in range(2):
                w = wtile(l, f"CONVOUT_{i}")
                for j in range(4):
                    c = 4 * i + j
                    b = fm_group(w, 8, j, lambda k: cvo(k), T)
                    t_ = tmp(T)
                    tt(t_, b.v(0, T), G.v(c * 512, c * 512 + T), ALU.mult)
                    tt(t_, M.v(c * 512, c * 512 + T), t_, ALU.add, eng=meng)
                    act(mbv(c), t_, AF.Identity)
            dump('cvo', AB, 0, 4096, dz)
            dump('mb', AB, 4096, 8192, dz)
            wo = [None, None]

            def prod_wout(c):
                i, j = c // 4, c % 4
                if j == 0:
                    wo[i] = wtile(l, f"WOUT_{i}")
                return fm_group(wo[i], 8, j, lambda k: mbv(k), T)
            post_norm_residual(l, b0, nb, PG_MIXPOST, prod_wout)
            for c in range(8):
                dump(f'h1_{c}', H, c * NTOK + HB * 128, c * NTOK + HB * 128 + 512, dz)

        def ffn(l, b0, nb):
            T = nb * 128
            dz = (l == 0 and b0 == 0)
            meng = "POOL" if l >= POOL_FROM_LAYER else "DVE"
            PUMPN[0] = 1
            for k in range(2):
                load("p", VG.v(1024 + k * 512, 1024 + k * 512 + T), pT[l, k][:, (b0 + HB) * 128:(b0 + HB) * 128 + T])
                cp(AB.v(11264 + k * 512, 11264 + k * 512 + T), VG.v(1024 + k * 512, 1024 + k * 512 + T))
            rms_to_hn(l, b0, nb, PG_FFNPRE, dst=G)
            hn2 = lambda k: G.v(k * 512, k * 512 + T)
            actv = lambda j: AB.v(j * 512, j * 512 + T)
            for i in range(11):
                w = wtile(l, f"FFNIN_{i}")
                for j in range(2):
                    gb = fm_group(w, 8, j, hn2, T)
                    sg = tmp(T)
                    act(sg, gb.v(0, T), AF.Silu)
                    ub = fm_group(w, 8, 2 + j, hn2, T)
                    tt(actv(2 * i + j), ub.v(0, T), sg, ALU.mult)
            fb = {}

            def prod_ffn(c):
                cg, j = c // 4, c % 4
                if j == 0:
                    bk = [bank() for _ in range(4)]
                    for kg, (k0, kc) in enumerate(((0, 8), (8, 8), (16, 6))):
                        w = wtile(l, f"FFNOUT_{cg}_{kg}")
                        for jj in range(4):
                            for k in range(kc):
                                mm(bk[jj].v(0, T), w(k, jj * 128), actv(k0 + k), k0 + k == 0, k0 + k == 21)
                            pump(1)
                    fb[cg] = bk
                return fb[cg][j]
            post_norm_residual(l, b0, nb, PG_FFNPOST, prod_ffn)
            for c in range(8):
                dump(f'h2_{c}', H, c * NTOK + HB * 128, c * NTOK + HB * 128 + 512, dz)

        def ple(l, gl, b0, nb, is_last_layer):
            T = nb * 128
            dz = (l == 0 and b0 == 0)
            meng = "POOL" if l >= POOL_FROM_LAYER else "DVE"
            hb = lambda c: G.v(c * 512, c * 512 + T)
            for c in range(8):
                act(hb(c), hv(c, b0, nb), AF.Identity)
            pbv = lambda k: AB.v(11264 + k * 512, 11264 + k * 512 + T)
            for i in range(2):
                w = wtile(l, f"PLEG_{i}")
                for j in range(4):
                    c = 4 * i + j
                    b = fm_group(w, 8, j, lambda k: hb(k), T)
                    act(M.v(c * 512, c * 512 + T), b.v(0, T), AF.Sigmoid)
            w = wtile(l, "PLEW")
            for c in range(8):
                b = bank()
                for k in range(2):
                    mm(b.v(0, T), w(k, c * 128), pbv(k), k == 0, k == 1)
                t_ = tmp(T)
                tt(t_, b.v(0, T), M.v(c * 512, c * 512 + T), ALU.mult)
                tt(hv(c, b0, nb), hv(c, b0, nb), t_, ALU.add, eng=meng)
            for c in range(8):
                dump(f'h3_{c}', H, c * NTOK + HB * 128, c * NTOK + HB * 128 + 512, dz)
            if is_last_layer and b0 >= 0:
                for c in range(8):
                    hvv = hv(c, b0, nb)
                    P.dma("SP", f"out{c % 4}", (lambda e, hvv=hvv, c=c: e.dma_start(out=outT[c][:, b0 * 128:b0 * 128 + T], in_=hvv.ap)),
                          reads=[hvv])

        for l in range(nl):
            sgu_setup(l)
            pb = -(nl - l)
            halo = [(b, 1) for b in range(pb + 1, 0)]
            own = [(b0, 4) for b0 in range(0, NOWN, 4)]
            if halo and len(halo) == 1 and MERGE_HALO:
                sizes = (4, 4, 3, 3, 3)
                tiles_, b_ = [], -1
                for sz in sizes:
                    tiles_.append((b_, sz))
                    b_ += sz
                steps = [("P", pb)] + [("T", t) for t in tiles_]
            else:
                steps = [("P", -1)] + [("T", t) for t in own]
                if halo:
                    steps += [("P", pb)] + [("T", t) for t in halo]
            tl = [s_ for s_ in steps]
            assert tl[0][0] == "P"
            partial_block(l, tl[0][1])
            idx = 1
            pre_rms(l, *tl[idx][1])
            pre_b1(l, *tl[idx][1])
            start_conv(l, *tl[idx][1])
            uv_done = False
            while idx < len(tl):
                b0, nb = tl[idx][1]
                nxt = idx + 1
                has_partial = nxt < len(tl) and tl[nxt][0] == "P"
                nxt_tile = nxt + 1 if has_partial else nxt
                have_next = nxt_tile < len(tl)
                hoist = have_next and not has_partial
                if not uv_done:
                    mixer_uv(l, b0, nb)
                hook = (lambda t=tl[nxt_tile][1]: pre_rms(l, *t)) if hoist else None
                mixer_rest(l, b0, nb, hook)
                if has_partial:
                    partial_block(l, tl[nxt][1])
                if have_next:
                    if not hoist:
                        pre_rms(l, *tl[nxt_tile][1])
                    pre_b1(l, *tl[nxt_tile][1])
                    start_conv(l, *tl[nxt_tile][1])
                ffn(l, b0, nb)
                uv_done = False
                if hoist:
                    mixer_uv(l, *tl[nxt_tile][1])
                    uv_done = True
                ple(l, l, b0, nb, last and l == nl - 1)
                idx = nxt_tile
        print("[kernel] sbuf bytes remaining", nc.sbuf_bytes_remaining)
        import os as _os2
        if _os2.environ.get('KERNEL_NOSCHED') is None:
            est = P.schedule()
            print(f"[kernel] list-scheduled, model makespan {est:.0f} us")
        nops, nwait = P.finalize(st)
        import os as _os
        if _os.environ.get('KERNEL_TAGS'):
            import json as _json
            _json.dump([o.tag for o in P.ops if o.eng == 'PE'], open(_os.environ['KERNEL_TAGS'], 'w'))
        print(f"[kernel] ops={nops} waits={nwait}")
    return nc


def _fm(a):
    nt, nf = a.shape
    return np.ascontiguousarray(a.T.reshape(nf // 128, 128, nt))


def _pool_mats(start):
    out = np.zeros((4, 4, 128, 128), np.float32)
    tp = np.arange(128)[:, None]
    t = np.arange(128)[None, :]
    for g, w in enumerate((2, 4, 8, 16)):
        d = t - tp
        cur = ((d >= 0) & (d < w)).astype(np.float32) / w - (d == 0)
        dp = t + 128 - tp
        prev = ((dp >= 0) & (dp < w)).astype(np.float32) / w
        out[0, g] = cur
        out[1, g] = prev
        if start:
            cnt = np.minimum(np.arange(128) + 1, w).astype(np.float32)[None, :]
            out[2, g] = ((d >= 0) & (d < w)).astype(np.float32) / cnt - (d == 0)
            out[3, g] = 0.0
        else:
            out[2, g] = cur
            out[3, g] = prev
    return np.ascontiguousarray(out.transpose(2, 0, 1, 3).reshape(128, 16 * 128))


_NC_CACHE = {}


def _get_nc(nl):
    if nl not in _NC_CACHE:
        _NC_CACHE[nl] = build(nl, True)
    return _NC_CACHE[nl]


def _param_blob(inp, l):
    cols = []
    for n in ("g_mix_pre", "g_sgu_v", "b_sgu_v", "b_dw", "g_conv_ln", "b_conv_ln", "s_pool", "g_mix_post",
              "g_ffn_pre", "g_ffn_post"):
        cols.append(np.asarray(inp[n][l], np.float32).reshape(8, 128).T)
    wdw = np.asarray(inp["w_dw"][l], np.float32).reshape(31, 8, 128)
    cols.append(wdw.transpose(2, 1, 0).reshape(128, 8 * 31))
    return np.ascontiguousarray(np.concatenate(cols, axis=1))


def make_in_maps(inp, nl):
    HB = nl
    x = np.asarray(inp["x"], np.float32)
    p = np.asarray(inp["p"], np.float32)
    B, S, _ = x.shape
    nq = 8 // B
    SQ = S // nq
    i = np.arange(128)
    ch = i // 64
    maskT = (ch[:, None] <= ch[None, :]).astype(np.float32)
    shared = {wn: np.ascontiguousarray(np.asarray(inp[wn], np.float32).reshape([nl] + WSHAPES[wn])) for wn in WSHAPES}
    shared["pblob"] = np.stack([_param_blob(inp, l) for l in range(nl)])
    ws = np.asarray(inp["w_sgu_s"], np.float32)
    shared["wsT"] = np.ascontiguousarray(ws.transpose(0, 3, 1, 2).reshape(nl, 128, 1024))
    bs = np.asarray(inp["b_sgu_s"], np.float32).reshape(nl, 1, 1024)
    shared["bsb"] = np.ascontiguousarray(np.broadcast_to(bs, (nl, 128, 1024)))
    shared["maskT"] = maskT
    in_maps = []
    for core in range(8):
        b, q = core // nq, core % nq
        s0 = q * SQ
        lo = s0 - HB * 128
        xs = np.zeros((HB * 128 + SQ, D), np.float32)
        ps = np.zeros((nl, HB * 128 + SQ, 256), np.float32)
        if lo >= 0:
            xs[:] = x[b, lo:s0 + SQ]
            ps[:] = p[:, b, lo:s0 + SQ]
        else:
            xs[HB * 128:] = x[b, s0:s0 + SQ]
            ps[:, HB * 128:] = p[:, b, s0:s0 + SQ]
        m = dict(shared)
        m["xT"] = _fm(xs)
        m["pT"] = np.stack([_fm(ps[l]) for l in range(nl)])
        m["pmats"] = _pool_mats(q == 0)
        m["cmask"] = np.full((128, 1), 0.0 if q == 0 else 1.0, np.float32)
        in_maps.append(m)
    return in_maps, (B, S, nq, SQ)


def kernel(**inp):
    nl = 2
    in_maps, (B, S, nq, SQ) = make_in_maps(inp, nl)
    nc = _get_nc(nl)
    res = run_bass_kernel_spmd(nc, in_maps, core_ids=list(range(8)))
    out = np.zeros((B, S, D), np.float32)
    for core in range(8):
        b, q = core // nq, core % nq
        o = res.results[core]["outT"]
        out[b, q * SQ:(q + 1) * SQ] = o.reshape(D, SQ).T
    return out
```
